# Optimizing a Trainium2 kernel written in Bass

```python
import jax
import jax.numpy as jnp
from jax import lax
import numpy as np

D_MODEL = 1024
BATCH = 2
SEQ = 8192
DEPTH = 2

MIX_WIDTH = D_MODEL // 2
N_BRANCH = 3
NORM_EPS = 1e-6

SGU_GROUPS = 4
SGU_CHUNK = 128
SGU_WIDTH = MIX_WIDTH
SGU_GROUP_DIM = SGU_WIDTH // SGU_GROUPS

SWA_HEADS = 8
SWA_KV_HEADS = 2
SWA_HEAD_DIM = MIX_WIDTH // SWA_HEADS
SWA_GROUP = SWA_HEADS // SWA_KV_HEADS
WINDOW = 128
ROPE_THETA = 500000.0
ROPE_DIM = SWA_HEAD_DIM // 4

DN_HEADS = 4
DN_HEAD_DIM = MIX_WIDTH // DN_HEADS
DN_CONV = 4
DN_CHUNK = 64

D_FF = ((8 * D_MODEL // 3 + 255) // 256) * 256

IN_WIDTHS = (SGU_WIDTH, SGU_WIDTH,
             SWA_HEADS * SWA_HEAD_DIM, SWA_KV_HEADS * SWA_HEAD_DIM, SWA_KV_HEADS * SWA_HEAD_DIM,
             3 * MIX_WIDTH, MIX_WIDTH, DN_HEADS, DN_HEADS,
             N_BRANCH * D_MODEL)
IN_COLS = sum(IN_WIDTHS)

kernel_name = 'hybrid_gated_parallel_mixers'


def rmsnorm(x, g):
    xf = x.astype(jnp.float32)
    y = xf * lax.rsqrt(jnp.mean(xf * xf, axis=-1, keepdims=True) + NORM_EPS)
    return (y * g.astype(jnp.float32)).astype(x.dtype)


def layernorm(x, g, b):
    xf = x.astype(jnp.float32)
    xc = xf - jnp.mean(xf, axis=-1, keepdims=True)
    y = xc * lax.rsqrt(jnp.mean(xc * xc, axis=-1, keepdims=True) + NORM_EPS)
    return (y * g.astype(jnp.float32) + b.astype(jnp.float32)).astype(x.dtype)


def l2norm(x):
    return x * lax.rsqrt(jnp.sum(x * x, axis=-1, keepdims=True) + NORM_EPS)


def split_columns(t):
    parts, start = [], 0
    for w in IN_WIDTHS:
        parts.append(t[..., start:start + w])
        start += w
    return parts


def rotary_tables(positions):
    inv_freq = ROPE_THETA ** (-jnp.arange(0, ROPE_DIM, 2, dtype=jnp.float32) / ROPE_DIM)
    ang = positions.astype(jnp.float32)[..., None] * inv_freq
    return jnp.cos(ang)[:, :, None, :], jnp.sin(ang)[:, :, None, :]


def apply_partial_rope(x, cos, sin):
    half = ROPE_DIM // 2
    x1, x2, rest = x[..., :half], x[..., half:ROPE_DIM], x[..., ROPE_DIM:]
    c, s = cos.astype(x.dtype), sin.astype(x.dtype)
    return jnp.concatenate([x1 * c - x2 * s, x2 * c + x1 * s, rest], axis=-1)


def spatial_gating(u, v, ln_g, ln_b, w_s, b_s):
    B_, S_ = u.shape[:2]
    nc = S_ // SGU_CHUNK
    vn = layernorm(v, ln_g, ln_b).reshape(B_, nc, SGU_CHUNK, SGU_GROUPS, SGU_GROUP_DIM)
    causal = jnp.tril(jnp.ones((SGU_CHUNK, SGU_CHUNK), dtype=bool))
    w_causal = jnp.where(causal, w_s, 0.0).astype(vn.dtype)
    mixed = jnp.einsum('gts,bnsgc->bntgc', w_causal, vn) + b_s.T.astype(vn.dtype)[None, None, :, :, None]
    return u * mixed.reshape(B_, S_, SGU_WIDTH)


def sliding_window_attention(q, k, v, sinks, cos, sin):
    B_, S_ = q.shape[:2]
    nc = S_ // WINDOW
    q = apply_partial_rope(q, cos, sin) * (SWA_HEAD_DIM ** -0.5)
    k = apply_partial_rope(k, cos, sin)
    qb = q.reshape(B_, nc, WINDOW, SWA_KV_HEADS, SWA_GROUP, SWA_HEAD_DIM)

    def band(t):
        cur = t.reshape(B_, nc, WINDOW, SWA_KV_HEADS, SWA_HEAD_DIM)
        prev = jnp.concatenate([jnp.zeros_like(cur[:, :1]), cur[:, :-1]], axis=1)
        return jnp.concatenate([prev, cur], axis=2)

    kb, vb = band(k), band(v)
    logits = jnp.einsum('bnqkgd,bnskd->bnkgqs', qb, kb).astype(jnp.float32)
    qi = jnp.arange(WINDOW)[:, None]
    sj = jnp.arange(2 * WINDOW)[None, :]
    diff = qi + WINDOW - sj
    in_band = (diff >= 0) & (diff < WINDOW)
    valid = (jnp.arange(nc) > 0)[:, None, None] | (sj >= WINDOW)[None]
    mask = in_band[None] & valid
    logits = jnp.where(mask[None, :, None, None], logits, -jnp.inf)
    sink = jnp.broadcast_to(sinks.astype(jnp.float32).reshape(1, 1, SWA_KV_HEADS, SWA_GROUP, 1, 1),
                            logits.shape[:-1] + (1,))
    probs = jax.nn.softmax(jnp.concatenate([logits, sink], axis=-1), axis=-1)[..., :-1]
    out = jnp.einsum('bnkgqs,bnskd->bnqkgd', probs.astype(vb.dtype), vb)
    return out.reshape(B_, S_, SWA_HEADS * SWA_HEAD_DIM)


def causal_short_conv(x, w):
    S_ = x.shape[1]
    xp = jnp.pad(x, ((0, 0), (DN_CONV - 1, 0), (0, 0)))
    out = xp[:, 0:S_] * w[0]
    for i in range(1, DN_CONV):
        out = out + xp[:, i:i + S_] * w[i]
    return jax.nn.silu(out)


def gated_deltanet(qkv, z, beta_logit, a_logit, conv_w, a_log, dt_bias, norm_g):
    B_, S_ = qkv.shape[:2]
    in_dtype = qkv.dtype
    nt = S_ // DN_CHUNK
    H, hd, C = DN_HEADS, DN_HEAD_DIM, DN_CHUNK
    qkv = causal_short_conv(qkv, conv_w).astype(jnp.float32)
    q = l2norm(qkv[..., :MIX_WIDTH].reshape(B_, S_, H, hd)) * (hd ** -0.5)
    k = l2norm(qkv[..., MIX_WIDTH:2 * MIX_WIDTH].reshape(B_, S_, H, hd))
    v = qkv[..., 2 * MIX_WIDTH:].reshape(B_, S_, H, hd)
    beta = jax.nn.sigmoid(beta_logit.astype(jnp.float32))
    g = -jnp.exp(a_log.astype(jnp.float32)) * jax.nn.softplus(a_logit.astype(jnp.float32) + dt_bias.astype(jnp.float32))

    def to_chunks(t):
        t = t.reshape((B_, nt, C) + t.shape[2:])
        return jnp.swapaxes(jnp.swapaxes(t, 0, 1), 2, 3)

    q, k, v, beta, g = to_chunks(q), to_chunks(k), to_chunks(v), to_chunks(beta), to_chunks(g)
    gc = jnp.cumsum(g, axis=-1)
    tril = jnp.tril(jnp.ones((C, C), dtype=bool))
    strict = jnp.tril(jnp.ones((C, C), dtype=bool), -1)
    decay = jnp.exp(jnp.where(tril, gc[..., :, None] - gc[..., None, :], -jnp.inf))
    k_beta = k * beta[..., None]
    lower = jnp.where(strict, jnp.einsum('nbhid,nbhjd->nbhij', k_beta, k) * decay, 0.0)
    a_mat = lower + jnp.eye(C, dtype=jnp.float32)
    rhs = jnp.concatenate([v * beta[..., None], k_beta * jnp.exp(gc)[..., None]], axis=-1)
    sol = lax.linalg.triangular_solve(a_mat, rhs, left_side=True, lower=True, unit_diagonal=True)
    u_c, w_c = sol[..., :hd], sol[..., hd:]
    attn = jnp.einsum('nbhid,nbhjd->nbhij', q, k) * decay
    q_dec = q * jnp.exp(gc)[..., None]
    k_dec = k * jnp.exp(gc[..., -1:] - gc)[..., None]
    c_dec = jnp.exp(gc[..., -1])

    def step(state, xs):
        qd, wc, uc, at, kd, cd = xs
        v_new = uc - jnp.einsum('bhcd,bhde->bhce', wc, state)
        o_c = jnp.einsum('bhcd,bhde->bhce', qd, state) + jnp.einsum('bhij,bhje->bhie', at, v_new)
        state = state * cd[..., None, None] + jnp.einsum('bhcd,bhce->bhde', kd, v_new)
        return state, o_c

    state0 = jnp.zeros((B_, H, hd, hd), dtype=jnp.float32)
    _, o = lax.scan(step, state0, (q_dec, w_c, u_c, attn, k_dec, c_dec))
    o = jnp.swapaxes(jnp.swapaxes(o, 0, 1), 2, 3).reshape(B_, S_, H, hd)
    o = rmsnorm(o, norm_g) * jax.nn.silu(z.astype(jnp.float32).reshape(B_, S_, H, hd))
    return o.reshape(B_, S_, MIX_WIDTH).astype(in_dtype)


def setup_inputs(seed: int = 0) -> dict:
    key = jax.random.key(seed)
    ks = jax.random.split(key, 20)
    f32 = jnp.float32

    def nrm(k, shape, scale):
        return jax.random.normal(k, shape, dtype=f32) * scale

    dt = jnp.exp(jax.random.uniform(ks[11], (DEPTH, DN_HEADS), dtype=f32,
                                    minval=np.log(1e-3), maxval=np.log(1e-1)))
    return {
        'x': nrm(ks[0], (BATCH, SEQ, D_MODEL), 1.0),
        'positions': jnp.broadcast_to(jnp.arange(SEQ, dtype=jnp.int32), (BATCH, SEQ)),
        'attn_norm': 1.0 + nrm(ks[1], (DEPTH, D_MODEL), 0.02),
        'w_in': nrm(ks[2], (DEPTH, D_MODEL, IN_COLS), D_MODEL ** -0.5),
        'sgu_ln_g': 1.0 + nrm(ks[3], (DEPTH, SGU_WIDTH), 0.02),
        'sgu_ln_b': nrm(ks[4], (DEPTH, SGU_WIDTH), 0.02),
        'sgu_w': nrm(ks[5], (DEPTH, SGU_GROUPS, SGU_CHUNK, SGU_CHUNK), SGU_CHUNK ** -0.5),
        'sgu_b': 1.0 + nrm(ks[6], (DEPTH, SGU_GROUPS, SGU_CHUNK), 0.02),
        'attn_sinks': nrm(ks[7], (DEPTH, SWA_HEADS), 0.5),
        'dn_conv_w': nrm(ks[8], (DEPTH, DN_CONV, 3 * MIX_WIDTH), DN_CONV ** -0.5),
        'dn_a_log': jnp.log(jax.random.uniform(ks[9], (DEPTH, DN_HEADS), dtype=f32, minval=1.0, maxval=16.0)),
        'dn_dt_bias': dt + jnp.log(-jnp.expm1(-dt)),
        'dn_norm': 1.0 + nrm(ks[10], (DEPTH, DN_HEAD_DIM), 0.02),
        'w_branch': nrm(ks[12], (DEPTH, N_BRANCH, MIX_WIDTH, D_MODEL), MIX_WIDTH ** -0.5),
        'w_out': nrm(ks[13], (DEPTH, D_MODEL, D_MODEL), D_MODEL ** -0.5),
        'ffn_norm': 1.0 + nrm(ks[14], (DEPTH, D_MODEL), 0.02),
        'w_gate_up': nrm(ks[15], (DEPTH, D_MODEL, 2 * D_FF), D_MODEL ** -0.5),
        'w_down': nrm(ks[16], (DEPTH, D_FF, D_MODEL), D_FF ** -0.5),
        'final_norm': 1.0 + nrm(ks[17], (D_MODEL,), 0.02),
    }


def reference(x, positions, attn_norm, w_in, sgu_ln_g, sgu_ln_b, sgu_w, sgu_b, attn_sinks,
              dn_conv_w, dn_a_log, dn_dt_bias, dn_norm, w_branch, w_out, ffn_norm,
              w_gate_up, w_down, final_norm):
    B_, S_ = x.shape[:2]
    cos, sin = rotary_tables(positions)
    for layer in range(DEPTH):
        h = rmsnorm(x, attn_norm[layer])
        proj = jnp.einsum('bsd,dc->bsc', h, w_in[layer])
        u_a, v_a, q_b, k_b, v_b, qkv_c, z_c, beta_c, a_c, gate_pre = split_columns(proj)
        out_a = spatial_gating(jax.nn.gelu(u_a), jax.nn.gelu(v_a), sgu_ln_g[layer], sgu_ln_b[layer],
                               sgu_w[layer], sgu_b[layer])
        out_b = sliding_window_attention(q_b.reshape(B_, S_, SWA_HEADS, SWA_HEAD_DIM),
                                         k_b.reshape(B_, S_, SWA_KV_HEADS, SWA_HEAD_DIM),
                                         v_b.reshape(B_, S_, SWA_KV_HEADS, SWA_HEAD_DIM),
                                         attn_sinks[layer], cos, sin)
        out_c = gated_deltanet(qkv_c, z_c, beta_c, a_c, dn_conv_w[layer], dn_a_log[layer],
                               dn_dt_bias[layer], dn_norm[layer])
        branches = jnp.stack([out_a, out_b, out_c], axis=0)
        branch_d = jnp.einsum('nbsc,ncd->nbsd', branches, w_branch[layer])
        gates = jax.nn.sigmoid(gate_pre.reshape(B_, S_, N_BRANCH, D_MODEL))
        merged = jnp.einsum('bsnd,nbsd->bsd', gates, branch_d)
        x = x + jnp.einsum('bsd,de->bse', merged, w_out[layer])
        h2 = rmsnorm(x, ffn_norm[layer])
        gu = jnp.einsum('bsd,df->bsf', h2, w_gate_up[layer])
        x = x + jnp.einsum('bsf,fd->bsd', jax.nn.silu(gu[..., :D_FF]) * gu[..., D_FF:], w_down[layer])
    return rmsnorm(x, final_norm)
```

```python
import contextlib
import numpy as np
import concourse.bass as bass
import concourse.mybir as mybir
from concourse.bass_utils import run_bass_kernel_spmd

F32 = mybir.dt.float32
BF16 = mybir.dt.bfloat16
I32 = mybir.dt.int32
AF = mybir.ActivationFunctionType
ALU = mybir.AluOpType

ENGS = ("pe", "act", "dve", "pool", "sp")
EPOCH = 2000

D = 1024
DEPTH = 2
NCOL = 6920
DFF = 2816
EPS = 1e-6
TT = 512
NS = TT // 128


class Op:
    __slots__ = ("eng", "fn", "deps", "sig", "semi", "semv", "dma_key", "dma_val", "waits")

    def __init__(self, eng, fn):
        self.eng = eng
        self.fn = fn
        self.deps = ()
        self.sig = False
        self.semi = -1
        self.semv = 0
        self.dma_key = None
        self.dma_val = 0
        self.waits = None


class Prog:
    def __init__(self, nc):
        self.nc = nc
        self.ops = {e: [] for e in ENGS}
        self.last_w = {}
        self.readers = {}
        self.dma_cnt = {}
        self.stack = contextlib.ExitStack()
        self.ntile = 0
        self.bank_i = 0

    def sb(self, shape, dt, name=None):
        self.ntile += 1
        name = "sb_" + (name or f"t{self.ntile}")
        return self.stack.enter_context(self.nc.sbuf_tensor(name, list(shape), dt))

    def ps(self, shape, dt, name=None):
        self.ntile += 1
        name = name or f"p{self.ntile}"
        return self.stack.enter_context(self.nc.psum_tensor(name, list(shape), dt))

    def add(self, eng, fn, r=(), w=(), dma_key=None):
        op = Op(eng, fn)
        deps = set()
        bk_r = [k for k in r if isinstance(k, tuple) and k[0] == "bank"]
        if bk_r:
            r = [k for k in r if not (isinstance(k, tuple) and k[0] == "bank")]
            w = list(w) + bk_r
        for k in r:
            x = self.last_w.get(k)
            if x is not None:
                deps.add(x)
        for k in w:
            x = self.last_w.get(k)
            if x is not None:
                deps.add(x)
            for y in self.readers.get(k, ()):
                deps.add(y)
        deps.discard(op)
        for k in r:
            self.readers.setdefault(k, []).append(op)
        for k in w:
            self.last_w[k] = op
            self.readers[k] = []
        op.deps = deps
        if dma_key is not None:
            op.dma_key = dma_key
            self.dma_cnt[dma_key] = self.dma_cnt.get(dma_key, 0) + 16
            op.dma_val = self.dma_cnt[dma_key]
        self.ops[eng].append(op)
        return op

    def dma(self, q, out, in_, r=(), w=(), key=None):
        return self.add(q, lambda e: e.dma_start(out=out, in_=in_), r, w, dma_key=key)

    def mm(self, out, lhsT, rhs, start=True, stop=True, r=(), w=()):
        return self.add("pe", lambda e: e.matmul(out, lhsT, rhs, start=start, stop=stop), r, w)

    def act(self, out, in_, func, r=(), w=(), **kw):
        return self.add("act", lambda e: e.activation(out, in_, func, **kw), r, w)

    def tt(self, eng, out, a, b, op, r=(), w=()):
        return self.add(eng, lambda e: e.tensor_tensor(out, a, b, op), r, w)

    def ts(self, eng, out, a, s1, s2, op0, op1=None, r=(), w=()):
        if op1 is None:
            return self.add(eng, lambda e: e.tensor_scalar(out, a, s1, s2, op0), r, w)
        return self.add(eng, lambda e: e.tensor_scalar(out, a, s1, s2, op0, op1), r, w)

    def stt(self, eng, out, a, s, b, op0, op1, r=(), w=()):
        return self.add(eng, lambda e: e.scalar_tensor_tensor(out, a, s, b, op0, op1), r, w)

    def cp(self, eng, out, in_, r=(), w=()):
        if eng == "act":
            return self.add(eng, lambda e: e.copy(out, in_), r, w)
        return self.add(eng, lambda e: e.tensor_copy(out, in_), r, w)

    def memset(self, eng, ap, val, w=()):
        return self.add(eng, lambda e: e.memset(ap, val), (), w)

    def finalize(self):
        nc = self.nc
        for e in ENGS:
            for op in self.ops[e]:
                for d in op.deps:
                    if d.dma_key is None:
                        if d.eng == op.eng and d.eng == "pe":
                            continue
                        d.sig = True
        self.eng_sems = {e: [] for e in ENGS}
        for e in ENGS:
            cnt = 0
            for op in self.ops[e]:
                if op.sig and op.dma_key is None:
                    op.semi = cnt // EPOCH
                    op.semv = cnt % EPOCH + 1
                    cnt += 1
            for i in range((cnt + EPOCH - 1) // EPOCH):
                self.eng_sems[e].append(self.stack.enter_context(nc.semaphore(f"s_{e}_{i}")))
        self.dma_sems = {k: self.stack.enter_context(nc.semaphore(f"d_{i}"))
                         for i, k in enumerate(self.dma_cnt)}
        for e in ENGS:
            seen = {}
            seen_dma = {}
            for op in self.ops[e]:
                need = {}
                need_dma = {}
                for d in op.deps:
                    if d.dma_key is not None:
                        if need_dma.get(d.dma_key, 0) < d.dma_val:
                            need_dma[d.dma_key] = d.dma_val
                    else:
                        if d.eng == e and e == "pe":
                            continue
                        v = (d.semi, d.semv)
                        if need.get(d.eng, (-1, 0)) < v:
                            need[d.eng] = v
                waits = []
                for te, v in need.items():
                    if seen.get(te, (-1, 0)) >= v:
                        continue
                    seen[te] = v
                    waits.append((self.eng_sems[te][v[0]], v[1]))
                for k, v in need_dma.items():
                    if seen_dma.get(k, 0) >= v:
                        continue
                    seen_dma[k] = v
                    waits.append((self.dma_sems[k], v))
                op.waits = waits
        with nc.Block() as block:
            def run(ename):
                def body(eng):
                    for op in self.ops[ename]:
                        for (s, v) in op.waits:
                            eng.wait_ge(s, v)
                        if op.fn is None:
                            continue
                        ins = op.fn(eng)
                        if op.dma_key is not None:
                            ins.then_inc(self.dma_sems[op.dma_key], 16)
                        elif op.sig:
                            ins.then_inc(self.eng_sems[ename][op.semi], 1)
                return body
            block.tensor(run("pe"))
            block.scalar(run("act"))
            block.vector(run("dve"))
            block.gpsimd(run("pool"))
            block.sync(run("sp"))
        self.stack.close()


C_U, C_V, C_QB, C_KB, C_VB, C_QKVC, C_Z, C_BETA, C_A, C_GATE = 0, 512, 1024, 1536, 1664, 1792, 3328, 3840, 3844, 3848

PB_ANORM = 0
PB_FNORM = 8
PB_LNG = 16
PB_LNB = 528
PB_SGUB = 1040
PB_DNN = 1552
PB_SINK = 1680
PB_ALOG = 1688
PB_DTB = 1692
PB_CONV = 1696
NPAR = 1744

CB_ID = 0
CB_TRIL = 128
CB_TRIU = 256
CB_STRL = 384
CB_STRU = 512
CB_ONES = 640
CB_FREQ = 768
NCONST = 776


def build(NT):
    ntile = NT // TT
    nc = bass.Bass("TRN2", target_bir_lowering=False)
    x_d = nc.dram_tensor("x", [NT, D], F32, kind="ExternalInput").ap()
    pos_d = nc.dram_tensor("pos", [128, NT // 128], I32, kind="ExternalInput").ap()
    par_d = nc.dram_tensor("par", [DEPTH, 128, NPAR], F32, kind="ExternalInput").ap()
    fin_d = nc.dram_tensor("fin", [128, D], F32, kind="ExternalInput").ap()
    cst_d = nc.dram_tensor("cst", [128, NCONST], F32, kind="ExternalInput").ap()
    w_in_d = nc.dram_tensor("w_in", [DEPTH, D, NCOL], F32, kind="ExternalInput").ap()
    sguw_d = nc.dram_tensor("sgu_w", [DEPTH, 4, 128, 128], F32, kind="ExternalInput").ap()
    w_br_d = nc.dram_tensor("w_branch", [DEPTH, 3, 512, D], F32, kind="ExternalInput").ap()
    w_out_d = nc.dram_tensor("w_out", [DEPTH, D, D], F32, kind="ExternalInput").ap()
    w_gu_d = nc.dram_tensor("w_gate_up", [DEPTH, D, 2 * DFF], F32, kind="ExternalInput").ap()
    w_dn_d = nc.dram_tensor("w_down", [DEPTH, DFF, D], F32, kind="ExternalInput").ap()
    y_d = nc.dram_tensor("y", [NT, D], F32, kind="ExternalOutput").ap()

    P = Prog(nc)
    cst = P.sb([128, NCONST], F32, "cst")
    par = [P.sb([128, NPAR], F32, f"par{l}") for l in range(DEPTH)]
    fin = P.sb([128, D], F32, "fin")
    idb = P.sb([128, 128], BF16, "idb")
    posf = P.sb([128, NT // 128], F32, "posf")
    posi = P.sb([128, NT // 128], I32, "posi")
    cosT = P.sb([128, NT // 128, 8], F32, "cosT")
    sinT = P.sb([128, NT // 128, 8], F32, "sinT")
    sguWT = [P.sb([128, 4, 128], BF16, f"sguWT{l}") for l in range(DEPTH)]
    esink = [P.sb([128, 8], F32, f"esink{l}") for l in range(DEPTH)]
    negA = [P.sb([128, 4], F32, f"negA{l}") for l in range(DEPTH)]
    S_st = [P.sb([128, 4, 128], F32, f"S{l}") for l in range(DEPTH)]
    kTp = [P.sb([128, 128], BF16, f"kTp{l}") for l in range(DEPTH)]
    vp = [P.sb([128, 2, 65], BF16, f"vp{l}") for l in range(DEPTH)]
    ctail = [P.sb([128, 12, 3], F32, f"ct{l}") for l in range(DEPTH)]
    x_tm = P.sb([128, NS, D], F32, "x_tm")
    hbf = P.sb([128, D], BF16, "hbf")
    hT = P.sb([128, 8, TT], BF16, "hT")
    junk = P.sb([128, D], F32, "junk")
    st = P.sb([128, 16], F32, "st")
    wbuf = [P.sb([128, 8, 512], BF16, f"wbuf{i}") for i in range(3)]
    uT = P.sb([128, 4, TT], BF16, "uT")
    vn = P.sb([128, 512], BF16, "vn")
    oaT = P.sb([128, 4, TT], BF16, "oaT")
    qkvb = P.sb([128, 768], F32, "qkvb")
    rtmp = P.sb([128, 10, 8], F32, "rtmp")
    qTb = P.sb([128, 4, 128], BF16, "qTb")
    kTb = P.sb([128, 128], BF16, "kTb")
    vb = P.sb([128, 2, 65], BF16, "vb")
    pT = P.sb([128, 2, 128], BF16, "pTs")
    pf = P.sb([128, 2, 128], F32, "pf")
    obn = P.sb([128, 8, 65], F32, "obn")
    ob = P.sb([128, 512], BF16, "ob")
    obT = P.sb([128, 4, TT], BF16, "obT")
    qkvc = P.sb([128, 3 + TT], F32, "qkvc")
    cacc = P.sb([128, TT], F32, "cacc")
    qkvs = P.sb([128, 12, TT], F32, "qkvs")
    zba = P.sb([128, NS, 520], F32, "zba")
    gt = P.sb([128, 16], F32, "gt")
    vtm = cacc
    dn = {n: P.sb([128, 128], F32, "dn_" + n) for n in
          ["ktm", "cacc", "rhsg", "dec", "decT", "L", "M", "X", "XT", "TT", "Y", "vnew", "kdec", "attT", "o", "tmp"]}
    dsm = P.sb([128, 8], F32, "dsm")
    ocb = P.sb([128, 512], BF16, "ocb")
    ocT = P.sb([128, 4, TT], BF16, "ocT")
    gsig = P.sb([128, TT], F32, "gsig")
    sg = gsig
    m32 = P.sb([128, 8, TT], F32, "m32")
    mT = P.sb([128, 8, TT], BF16, "mT")
    actT = P.sb([128, 22, TT], BF16, "actT")
    banks = [P.ps([128, 512], F32, f"bank{i}") for i in range(8)]

    def bank():
        i = P.bank_i
        P.bank_i = (i + 1) % 8
        return banks[i], ("bank", i)

    wrot = [0]

    def wload(src_ap, ncols, kparts=8):
        i = wrot[0]
        wrot[0] = (i + 1) % 3
        t = wbuf[i]
        P.dma("pool", t[:, 0:kparts, 0:ncols], src_ap.rearrange("(k p) c -> p k c", p=128),
              w=[("wbuf", i)], key=("wbuf", i))
        return t, ("wbuf", i)

    ident = cst[:, CB_ID:CB_ID + 128]

    P.dma("sp", cst[:], cst_d[:, :], w=["cst"], key="cst")
    for l in range(DEPTH):
        P.dma("sp", par[l][:], par_d[l], w=[("par", l)], key=("par", l))
    P.dma("sp", fin[:], fin_d[:, :], w=["fin"], key="fin")
    P.dma("sp", posi[:], pos_d[:, :], w=["posi"], key="posi")
    P.cp("dve", idb[:], ident, r=["cst"], w=["idb"])
    P.cp("dve", posf[:], posi[:], r=["posi"], w=["posf"])
    TWO_PI = 2.0 * np.pi
    nsub = NT // 128
    for which, shift, dst in (("s", np.pi, sinT), ("c", 1.5 * np.pi, cosT)):
        for j in range(nsub):
            a = rtmp[:, 0, :]
            k_f = rtmp[:, 1, :]
            k_i = rtmp[:, 2, :].bitcast(I32)
            fix = rtmp[:, 3, :]
            P.ts("dve", a, cst[:, CB_FREQ:CB_FREQ + 8], posf[:, j:j + 1], float(shift), ALU.mult, ALU.add,
                 r=["cst", "posf"], w=["rtmp"])
            P.ts("dve", k_f, a, 1.0 / TWO_PI, None, ALU.mult, r=["rtmp"], w=["rtmp"])
            P.cp("dve", k_i, k_f, r=["rtmp"], w=["rtmp"])
            P.cp("dve", k_f, k_i, r=["rtmp"], w=["rtmp"])
            P.stt("dve", a, k_f, -TWO_PI, a, ALU.mult, ALU.add, r=["rtmp"], w=["rtmp"])
            P.ts("dve", fix, a, 0.0, TWO_PI, ALU.is_lt, ALU.mult, r=["rtmp"], w=["rtmp"])
            P.tt("dve", a, a, fix, ALU.add, r=["rtmp"], w=["rtmp"])
            P.ts("dve", fix, a, TWO_PI, -TWO_PI, ALU.is_ge, ALU.mult, r=["rtmp"], w=["rtmp"])
            P.tt("dve", a, a, fix, ALU.add, r=["rtmp"], w=["rtmp"])
            P.ts("dve", a, a, -np.pi, None, ALU.add, r=["rtmp"], w=["rtmp"])
            P.ts("dve", a, a, -3.1415925, 3.1415925, ALU.max, ALU.min, r=["rtmp"], w=["rtmp"])
            P.act(dst[:, j, :], a, AF.Sin, r=["rtmp"], w=[("rope", which)])
    for l in range(DEPTH):
        wtmp = dn["tmp"]
        for g in range(4):
            P.dma("sp", wtmp[:], sguw_d[l, g], w=["dn_tmp"], key="dn_tmp")
            P.tt("dve", wtmp[:], wtmp[:], cst[:, CB_TRIL:CB_TRIL + 128], ALU.mult, r=["dn_tmp", "cst"], w=["dn_tmp"])
            bk, bkk = bank()
            P.mm(bk[:, 0:128], wtmp[:], ident, r=["dn_tmp", "cst"], w=[bkk])
            P.cp("dve", sguWT[l][:, g, :], bk[:, 0:128], r=[bkk], w=[("sguWT", l)])
        P.act(esink[l][:], par[l][:, PB_SINK:PB_SINK + 8], AF.Exp, r=[("par", l)], w=[("esink", l)])
        P.act(negA[l][:], par[l][:, PB_ALOG:PB_ALOG + 4], AF.Exp, r=[("par", l)], w=[("negA", l)])
        P.ts("dve", negA[l][:], negA[l][:], -1.0, None, ALU.mult, r=[("negA", l)], w=[("negA", l)])
        P.memset("dve", S_st[l][:], 0.0, w=[("S", l, h) for h in range(4)])
        P.memset("dve", ctail[l][:], 0.0, w=[("ctail", l)])
        P.memset("dve", kTp[l][:], 0.0, w=[("kTp", l)])
        P.memset("dve", vp[l][:], 0.0, w=[("vp", l)])

    def rmsnorm_to_hT(gcol, l):
        for j in range(NS):
            P.act(junk[:], x_tm[:, j, :], AF.Square, r=[("x", j)], w=["junk", "st"], accum_out=st[:, 0:1], scale=1.0 / 32)
            P.act(st[:, 1:2], st[:, 0:1], AF.Sqrt, r=["st"], w=["st"], bias=EPS)
            P.add("dve", lambda e: e.reciprocal(st[:, 1:2], st[:, 1:2]), r=["st"], w=["st"])
            P.ts("dve", hbf[:], x_tm[:, j, :], st[:, 1:2], None, ALU.mult, r=[("x", j), "st"], w=["hbf"])
            for half in range(2):
                bk, bkk = bank()
                for q in range(4):
                    k = half * 4 + q
                    P.mm(bk[:, q * 128:(q + 1) * 128], hbf[:, k * 128:(k + 1) * 128], idb[:], r=["hbf", "idb"], w=[bkk])
                for q in range(4):
                    k = half * 4 + q
                    eng = "dve" if q % 2 == 0 else "pool"
                    eng = "dve"
                    P.ts(eng, hT[:, k, j * 128:(j + 1) * 128], bk[:, q * 128:(q + 1) * 128],
                         par[l][:, gcol + k:gcol + k + 1], None, ALU.mult, r=[bkk, ("par", l)], w=[("hT", j)])

    hT_all = [("hT", j) for j in range(NS)]

    def proj_fm(wt, wk, c, bk, bkk, ncol=128, kparts=8, rhs=None, rkeys=None):
        rhs = hT if rhs is None else rhs
        rkeys = hT_all if rkeys is None else rkeys
        for k in range(kparts):
            P.mm(bk[0:ncol, 0:TT], wt[:, k, c:c + ncol], rhs[:, k, :], start=(k == 0), stop=(k == kparts - 1),
                 r=[wk] + rkeys, w=[bkk])

    def proj_tm(wt, wk, j, c, n, bk, bkk, kparts=8, lhs=None, lkeys=None):
        lhs = hT if lhs is None else lhs
        lkeys = [("hT", j)] if lkeys is None else lkeys
        for k in range(kparts):
            P.mm(bk[:, 0:n], lhs[:, k, j * 128:(j + 1) * 128], wt[:, k, c:c + n], start=(k == 0), stop=(k == kparts - 1),
                 r=[wk] + lkeys, w=[bkk])

    def transpose_to(dst_fn, src, nblk, skeys, dkeys):
        bk, bkk = bank()
        for q in range(nblk):
            P.mm(bk[:, q * 128:(q + 1) * 128], src[:, q * 128:(q + 1) * 128], idb[:], r=skeys + ["idb"], w=[bkk])
        for q in range(nblk):
            P.cp("act" if q % 2 else "dve", dst_fn(q), bk[:, q * 128:(q + 1) * 128], r=[bkk], w=dkeys)

    for ti in range(ntile):
        t0 = ti * TT
        for j in range(NS):
            P.dma("sp", x_tm[:, j, :], x_d[t0 + j * 128:t0 + (j + 1) * 128, :], w=[("x", j)], key=("x", j))
        for l in range(DEPTH):
            pl = par[l]
            pk = ("par", l)
            w_in_l = w_in_d[l]
            rmsnorm_to_hT(PB_ANORM, l)
            wt, wk = wload(w_in_l[:, C_U:C_U + 512], 512)
            for c in range(4):
                bk, bkk = bank()
                proj_fm(wt, wk, c * 128, bk, bkk)
                P.act(uT[:, c, :], bk[:, 0:TT], AF.Gelu_apprx_tanh, r=[bkk], w=[("uT", c)])
            wt, wk = wload(w_in_l[:, C_V:C_V + 512], 512)
            for j in range(NS):
                bk, bkk = bank()
                proj_tm(wt, wk, j, 0, 512, bk, bkk)
                P.act(vtm[:], bk[:, 0:512], AF.Gelu_apprx_tanh, r=[bkk], w=["cacc", "st"], accum_out=st[:, 2:3])
                P.act(junk[:, 0:512], vtm[:], AF.Square, r=["cacc"], w=["junk", "st"], accum_out=st[:, 3:4])
                P.ts("dve", st[:, 4:5], st[:, 2:3], 1.0 / 512, None, ALU.mult, r=["st"], w=["st"])
                P.tt("dve", st[:, 5:6], st[:, 4:5], st[:, 4:5], ALU.mult, r=["st"], w=["st"])
                P.stt("dve", st[:, 6:7], st[:, 3:4], 1.0 / 512, st[:, 5:6], ALU.mult, ALU.subtract, r=["st"], w=["st"])
                P.act(st[:, 6:7], st[:, 6:7], AF.Sqrt, r=["st"], w=["st"], bias=EPS)
                P.add("dve", lambda e: e.reciprocal(st[:, 6:7], st[:, 6:7]), r=["st"], w=["st"])
                P.stt("dve", st[:, 7:8], st[:, 4:5], -1.0, st[:, 6:7], ALU.mult, ALU.mult, r=["st"], w=["st"])
                P.act(vtm[:], vtm[:], AF.Identity, r=["cacc", "st"], w=["cacc"], scale=st[:, 6:7], bias=st[:, 7:8])
                P.tt("dve", vtm[:], vtm[:], pl[:, PB_LNG:PB_LNG + 512], ALU.mult, r=["cacc", pk], w=["cacc"])
                P.tt("dve", vn[:], vtm[:], pl[:, PB_LNB:PB_LNB + 512], ALU.add, r=["cacc", pk], w=["vn"])
                bk, bkk = bank()
                for g in range(4):
                    P.mm(bk[:, g * 128:(g + 1) * 128], vn[:, g * 128:(g + 1) * 128], sguWT[l][:, g, :],
                         r=["vn", ("sguWT", l)], w=[bkk])
                for g in range(4):
                    P.tt("dve", junk[:, 0:128], bk[:, g * 128:(g + 1) * 128], pl[:, PB_SGUB + g * 128:PB_SGUB + (g + 1) * 128],
                         ALU.add, r=[bkk, pk], w=["junk"])
                    P.tt("dve", oaT[:, g, j * 128:(j + 1) * 128], junk[:, 0:128], uT[:, g, j * 128:(j + 1) * 128], ALU.mult,
                         r=["junk", ("uT", g)], w=[("oaT", g)])
            wt, wk = wload(w_in_l[:, C_QB:C_QB + 512], 512)
            wt2, wk2 = wload(w_in_l[:, C_KB:C_KB + 256], 256)
            for j in range(NS):
                gj = ti * NS + j
                bk, bkk = bank()
                proj_tm(wt, wk, j, 0, 512, bk, bkk)
                P.cp("act", qkvb[:, 0:512], bk[:, 0:512], r=[bkk], w=["qkvb"])
                bk, bkk = bank()
                proj_tm(wt2, wk2, j, 0, 256, bk, bkk)
                P.cp("act", qkvb[:, 512:768], bk[:, 0:256], r=[bkk], w=["qkvb"])
                qk = qkvb[:, 0:640].rearrange("p (h d) -> p h d", d=64)
                x1 = qk[:, :, 0:8]
                x2 = qk[:, :, 8:16]
                cb = cosT[:, gj:gj + 1, :].to_broadcast([128, 10, 8])
                sb_ = sinT[:, gj:gj + 1, :].to_broadcast([128, 10, 8])
                rk = ["qkvb", ("rope", "s"), ("rope", "c")]
                t1 = rtmp[:, :, :]
                P.tt("dve", t1, x1, sb_, ALU.mult, r=rk, w=["rtmp"])
                P.tt("dve", x1, x1, cb, ALU.mult, r=rk, w=["qkvb"])
                t2 = junk[:, 0:80].rearrange("p (h d) -> p h d", d=8)
                P.tt("dve", t2, x2, sb_, ALU.mult, r=rk, w=["junk"])
                P.tt("dve", x1, x1, t2, ALU.subtract, r=["qkvb", "junk"], w=["qkvb"])
                P.tt("dve", x2, x2, cb, ALU.mult, r=rk, w=["qkvb"])
                P.tt("dve", x2, x2, t1, ALU.add, r=["qkvb", "rtmp"], w=["qkvb"])
                qb16 = ob[:, :]
                for c in range(4):
                    P.cp("dve", qb16[:, c * 128:c * 128 + 64], qkvb[:, c * 64:(c + 1) * 64], r=["qkvb"], w=["ob"])
                    P.cp("dve", qb16[:, c * 128 + 64:(c + 1) * 128], qkvb[:, (c + 4) * 64:(c + 5) * 64], r=["qkvb"], w=["ob"])
                kb16 = ocb[:, 0:128]
                P.cp("dve", kb16, qkvb[:, 512:640], r=["qkvb"], w=["ocb"])
                P.memset("dve", vb[:, :, 64:65], 1.0, w=["vb"])
                P.cp("dve", vb[:, :, 0:64], qkvb[:, 640:768].rearrange("p (h d) -> p h d", d=64), r=["qkvb"], w=["vb"])
                transpose_to(lambda q: qTb[:, q, :], qb16, 4, ["ob"], ["qTb"])
                transpose_to(lambda q: kTb[:, :], kb16, 1, ["ocb"], ["kTb"])
                for h in range(8):
                    kv = h // 4
                    c = h % 4
                    pb = kv * 64
                    bk, bkk = bank()
                    P.mm(bk[:, 0:128], kTb[pb:pb + 64, :], qTb[pb:pb + 64, c, :], r=["kTb", "qTb"], w=[bkk])
                    nb = 1
                    if gj > 0:
                        P.mm(bk[:, 128:256], kTp[l][pb:pb + 64, :], qTb[pb:pb + 64, c, :], r=[("kTp", l), "qTb"], w=[bkk])
                        nb = 2
                    P.act(pf[:, 0:nb, :], bk[:, 0:nb * 128].rearrange("p (b q) -> p b q", b=nb), AF.Exp, r=[bkk], w=["pf"], scale=0.125)
                    P.tt("dve", pT[:, 0, :], pf[:, 0, :], cst[:, CB_TRIU:CB_TRIU + 128], ALU.mult, r=["pf", "cst"], w=["pT"])
                    if nb == 2:
                        P.tt("dve", pT[:, 1, :], pf[:, 1, :], cst[:, CB_STRL:CB_STRL + 128], ALU.mult, r=["pf", "cst"], w=["pT"])
                    bk2, bkk2 = bank()
                    P.mm(bk2[:, 0:65], pT[:, 0, :], vb[:, kv, :], start=True, stop=(nb == 1), r=["pT", "vb"], w=[bkk2])
                    if nb == 2:
                        P.mm(bk2[:, 0:65], pT[:, 1, :], vp[l][:, kv, :], start=False, stop=True, r=["pT", ("vp", l)], w=[bkk2])
                    P.cp("act", obn[:, h, :], bk2[:, 0:65], r=[bkk2], w=["obn"])
                P.tt("dve", st[:, 8:16], obn[:, :, 64], esink[l][:], ALU.add, r=["obn", ("esink", l)], w=["st"])
                P.add("dve", lambda e: e.reciprocal(st[:, 8:16], st[:, 8:16]), r=["st"], w=["st"])
                P.tt("dve", ob[:, :].rearrange("p (h d) -> p h d", d=64), obn[:, :, 0:64],
                     st[:, 8:16].unsqueeze(2).to_broadcast([128, 8, 64]), ALU.mult, r=["obn", "st"], w=["ob"])
                transpose_to(lambda q, j=j: obT[:, q, j * 128:(j + 1) * 128], ob, 4, ["ob"], [("obT", j)])
                P.cp("dve", kTp[l][:], kTb[:], r=["kTb"], w=[("kTp", l)])
                P.cp("dve", vp[l][:], vb[:], r=["vb"], w=[("vp", l)])
            for cg in range(3):
                wt, wk = wload(w_in_l[:, C_QKVC + cg * 512:C_QKVC + (cg + 1) * 512], 512)
                for c4 in range(4):
                    c = cg * 4 + c4
                    bk, bkk = bank()
                    proj_fm(wt, wk, c4 * 128, bk, bkk)
                    P.cp("dve", qkvc[:, 0:3], ctail[l][:, c, :], r=[("ctail", l)], w=["qkvc"])
                    P.cp("act", qkvc[:, 3:3 + TT], bk[:, 0:TT], r=[bkk], w=["qkvc"])
                    P.cp("dve", ctail[l][:, c, :], qkvc[:, TT:TT + 3], r=["qkvc"], w=[("ctail", l)])
                    cw = PB_CONV + c * 4
                    P.ts("dve", cacc[:], qkvc[:, 0:TT], pl[:, cw:cw + 1], None, ALU.mult, r=["qkvc", pk], w=["cacc"])
                    for i in range(1, 4):
                        P.stt("dve", cacc[:], qkvc[:, i:i + TT], pl[:, cw + i:cw + i + 1], cacc[:], ALU.mult, ALU.add,
                              r=["qkvc", pk, "cacc"], w=["cacc"])
                    P.act(qkvs[:, c, :], cacc[:], AF.Silu, r=["cacc"], w=[("qkvs", c)])
                    if c < 8:
                        P.act(cacc[:], qkvs[:, c, :], AF.Square, r=[("qkvs", c)], w=["cacc"])
                        bk, bkk = bank()
                        P.mm(bk[:, 0:TT], cst[:, CB_ONES:CB_ONES + 128], cacc[:], r=["cst", "cacc"], w=[bkk])
                        P.act(cacc[:], bk[:, 0:TT], AF.Sqrt, r=[bkk], w=["cacc"], bias=EPS)
                        P.add("dve", lambda e: e.reciprocal(cacc[:], cacc[:]), r=["cacc"], w=["cacc"])
                        if c < 4:
                            P.stt("dve", qkvs[:, c, :], qkvs[:, c, :], float(128 ** -0.5), cacc[:], ALU.mult, ALU.mult,
                                  r=[("qkvs", c), "cacc"], w=[("qkvs", c)])
                        else:
                            P.tt("dve", qkvs[:, c, :], qkvs[:, c, :], cacc[:], ALU.mult, r=[("qkvs", c), "cacc"], w=[("qkvs", c)])
            wt, wk = wload(w_in_l[:, C_Z:C_Z + 512], 512)
            wt2, wk2 = wload(w_in_l[:, C_BETA:C_BETA + 8], 8)
            for j in range(NS):
                bk, bkk = bank()
                proj_tm(wt, wk, j, 0, 512, bk, bkk)
                P.act(zba[:, j, 0:512], bk[:, 0:512], AF.Silu, r=[bkk], w=[("zba", j)])
                bk, bkk = bank()
                proj_tm(wt2, wk2, j, 0, 8, bk, bkk)
                P.cp("act", zba[:, j, 512:520], bk[:, 0:8], r=[bkk], w=[("zba", j)])
            for j in range(NS):
                tsl = slice(j * 128, (j + 1) * 128)
                P.act(gt[:, 0:4], zba[:, j, 512:516], AF.Sigmoid, r=[("zba", j)], w=["gt"])
                P.tt("dve", gt[:, 4:8], zba[:, j, 516:520], pl[:, PB_DTB:PB_DTB + 4], ALU.add, r=[("zba", j), pk], w=["gt"])
                P.act(gt[:, 4:8], gt[:, 4:8], AF.Exp, r=["gt"], w=["gt"])
                P.act(gt[:, 4:8], gt[:, 4:8], AF.Ln, r=["gt"], w=["gt"], bias=1.0)
                P.tt("dve", gt[:, 4:8], gt[:, 4:8], negA[l][:], ALU.mult, r=["gt", ("negA", l)], w=["gt"])
                bk, bkk = bank()
                P.mm(bk[:, 0:4], cst[:, CB_TRIU:CB_TRIU + 128], gt[:, 4:8], r=["cst", "gt"], w=[bkk])
                P.cp("dve", gt[:, 8:12], bk[:, 0:4], r=[bkk], w=["gt"])
                P.act(gt[:, 12:16], gt[:, 8:12], AF.Exp, r=["gt"], w=["gt"])
                for h in range(4):
                    qT_h = qkvs[:, h, tsl]
                    kT_h = qkvs[:, 4 + h, tsl]
                    vT_h = qkvs[:, 8 + h, tsl]
                    kq = [("qkvs", h), ("qkvs", 4 + h), ("qkvs", 8 + h)]
                    beta = gt[:, h:h + 1]
                    gc = gt[:, 8 + h:9 + h]
                    egc = gt[:, 12 + h:13 + h]
                    bk, bkk = bank()
                    P.mm(bk[:, 0:128], kT_h, ident, r=kq + ["cst"], w=[bkk])
                    P.mm(bk[:, 128:256], vT_h, ident, r=kq + ["cst"], w=[bkk])
                    P.cp("act", dn["ktm"][:], bk[:, 0:128], r=[bkk], w=["dn_ktm"])
                    P.cp("act", dn["cacc"][:], bk[:, 128:256], r=[bkk], w=["dn_vtm"])
                    P.ts("dve", dn["rhsg"][:], cst[:, CB_TRIU:CB_TRIU + 128], gt[:, 4 + h:5 + h], None, ALU.mult,
                         r=["cst", "gt"], w=["dn_rhsg"])
                    bkb, bkkb = bank()
                    P.mm(bkb[:, 0:128], cst[:, CB_ONES:CB_ONES + 128], dn["rhsg"][:], r=["cst", "dn_rhsg"], w=[bkkb])
                    P.ts("dve", dn["dec"][:], bkb[:, 0:128], gc, 0.0, ALU.subtract, ALU.max, r=[bkkb, "gt"], w=["dn_dec"])
                    P.ts("dve", dn["decT"][:], bkb[:, 0:128], gc, 0.0, ALU.subtract, ALU.min, r=[bkkb, "gt"], w=["dn_decT"])
                    P.act(dn["dec"][:], dn["dec"][:], AF.Exp, r=["dn_dec"], w=["dn_dec"], scale=-1.0)
                    P.act(dn["decT"][:], dn["decT"][:], AF.Exp, r=["dn_decT"], w=["dn_decT"])
                    P.act(dsm[:, 0:1], bkb[:, 127:128], AF.Exp, r=[bkkb], w=["dsm"])
                    P.ts("dve", dsm[:, 1:2], bkb[:, 127:128], gc, None, ALU.subtract, r=[bkkb, "gt"], w=["dsm"])
                    P.act(dsm[:, 1:2], dsm[:, 1:2], AF.Exp, r=["dsm"], w=["dsm"])
                    P.ts("dve", dsm[:, 2:3], egc, -1.0, None, ALU.mult, r=["gt"], w=["dsm"])
                    P.ts("dve", dn["kdec"][:], dn["ktm"][:], dsm[:, 1:2], None, ALU.mult, r=["dn_ktm", "dsm"], w=["dn_kdec"])
                    bk, bkk = bank()
                    P.mm(bk[:, 0:128], kT_h, kT_h, r=kq, w=[bkk])
                    P.mm(bk[:, 128:256], kT_h, qT_h, r=kq, w=[bkk])
                    P.tt("dve", dn["dec"][:], dn["dec"][:], cst[:, CB_STRL:CB_STRL + 128], ALU.mult, r=["dn_dec", "cst"], w=["dn_dec"])
                    P.stt("dve", dn["L"][:], bk[:, 0:128], beta, dn["dec"][:], ALU.mult, ALU.mult, r=[bkk, "gt", "dn_dec"], w=["dn_L"])
                    P.tt("dve", dn["decT"][:], dn["decT"][:], cst[:, CB_TRIU:CB_TRIU + 128], ALU.mult, r=["dn_decT", "cst"], w=["dn_decT"])
                    P.tt("dve", dn["attT"][:], bk[:, 128:256], dn["decT"][:], ALU.mult, r=[bkk, "dn_decT"], w=["dn_attT"])
                    bk, bkk = bank()
                    P.mm(bk[:, 0:128], dn["L"][:], ident, r=["dn_L", "cst"], w=[bkk])
                    P.cp("act", dn["XT"][:], bk[:, 0:128], r=[bkk], w=["dn_XT"])
                    P.tt("dve", dn["TT"][:], ident, bk[:, 0:128], ALU.subtract, r=["cst", bkk], w=["dn_TT"])
                    X, XT = dn["L"], dn["XT"]
                    xk, xtk = "dn_L", "dn_XT"
                    X2, XT2 = dn["X"], dn["M"]
                    x2k, xt2k = "dn_X", "dn_M"
                    for step in range(6):
                        bk, bkk = bank()
                        P.mm(bk[:, 0:128], XT[:], X[:], r=[xk, xtk], w=[bkk])
                        P.mm(bk[:, 128:256], X[:], XT[:], r=[xk, xtk], w=[bkk])
                        P.cp("act", X2[:], bk[:, 0:128], r=[bkk], w=[x2k])
                        if step < 5:
                            P.cp("dve", XT2[:], bk[:, 128:256], r=[bkk], w=[xt2k])
                        bk3, bkk3 = bank()
                        P.mm(bk3[:, 0:128], X2[:], dn["TT"][:], r=[x2k, "dn_TT"], w=[bkk3])
                        P.tt("dve", dn["TT"][:], dn["TT"][:], bk3[:, 0:128], ALU.add, r=["dn_TT", bkk3], w=["dn_TT"])
                        X, XT, X2, XT2 = X2, XT2, X, XT
                        xk, xtk, x2k, xt2k = x2k, xt2k, xk, xtk
                    Sh = S_st[l][:, h, :]
                    sk = ("S", l, h)
                    bk, bkk = bank()
                    P.mm(bk[:, 0:128], kT_h, Sh, r=kq + [sk], w=[bkk])
                    P.mm(bk[:, 128:256], qT_h, Sh, r=kq + [sk], w=[bkk])
                    P.stt("dve", dn["Y"][:], bk[:, 0:128], dsm[:, 2:3], dn["cacc"][:], ALU.mult, ALU.add,
                          r=[bkk, "dsm", "dn_vtm"], w=["dn_Y"])
                    P.ts("dve", dn["Y"][:], dn["Y"][:], beta, None, ALU.mult, r=["dn_Y", "gt"], w=["dn_Y"])
                    P.ts("dve", dn["o"][:], bk[:, 128:256], egc, None, ALU.mult, r=[bkk, "gt"], w=["dn_o"])
                    bk, bkk = bank()
                    P.mm(bk[:, 0:128], dn["TT"][:], dn["Y"][:], r=["dn_TT", "dn_Y"], w=[bkk])
                    P.cp("act", dn["vnew"][:], bk[:, 0:128], r=[bkk], w=["dn_vnew"])
                    bk, bkk = bank()
                    P.mm(bk[:, 0:128], dn["attT"][:], dn["vnew"][:], r=["dn_attT", "dn_vnew"], w=[bkk])
                    P.mm(bk[:, 128:256], dn["kdec"][:], dn["vnew"][:], r=["dn_kdec", "dn_vnew"], w=[bkk])
                    P.tt("dve", dn["o"][:], dn["o"][:], bk[:, 0:128], ALU.add, r=["dn_o", bkk], w=["dn_o"])
                    P.stt("dve", Sh, Sh, dsm[:, 0:1], bk[:, 128:256], ALU.mult, ALU.add, r=[sk, "dsm", bkk], w=[sk])
                    P.act(dn["tmp"][:], dn["o"][:], AF.Square, r=["dn_o"], w=["dn_tmp", "dsm"], accum_out=dsm[:, 3:4],
                          scale=float(128 ** -0.5))
                    P.act(dsm[:, 4:5], dsm[:, 3:4], AF.Sqrt, r=["dsm"], w=["dsm"], bias=EPS)
                    P.add("dve", lambda e: e.reciprocal(dsm[:, 4:5], dsm[:, 4:5]), r=["dsm"], w=["dsm"])
                    P.stt("dve", dn["o"][:], dn["o"][:], dsm[:, 4:5], pl[:, PB_DNN:PB_DNN + 128], ALU.mult, ALU.mult,
                          r=["dn_o", "dsm", pk], w=["dn_o"])
                    P.tt("dve", ocb[:, h * 128:(h + 1) * 128], dn["o"][:], zba[:, j, h * 128:(h + 1) * 128], ALU.mult,
                         r=["dn_o", ("zba", j)], w=["ocb"])
                transpose_to(lambda q, j=j: ocT[:, q, j * 128:(j + 1) * 128], ocb, 4, ["ocb"], [("ocT", j)])
            srcs = [(oaT, [("oaT", g) for g in range(4)]), (obT, [("obT", j) for j in range(NS)]),
                    (ocT, [("ocT", j) for j in range(NS)])]
            for br in range(3):
                srcT, skeys = srcs[br]
                for half in range(2):
                    wtb, wkb = wload(w_br_d[l, br][:, half * 512:(half + 1) * 512], 512, kparts=4)
                    wtg, wkg = wload(w_in_l[:, C_GATE + br * 1024 + half * 512:C_GATE + br * 1024 + (half + 1) * 512], 512)
                    for c4 in range(4):
                        m = half * 4 + c4
                        bk, bkk = bank()
                        proj_fm(wtg, wkg, c4 * 128, bk, bkk)
                        P.act(gsig[:], bk[:, 0:TT], AF.Sigmoid, r=[bkk], w=["gsig"])
                        bk, bkk = bank()
                        proj_fm(wtb, wkb, c4 * 128, bk, bkk, kparts=4, rhs=srcT, rkeys=skeys)
                        if br == 0:
                            P.tt("dve", m32[:, m, :], bk[:, 0:TT], gsig[:], ALU.mult, r=[bkk, "gsig"], w=[("m32", m)])
                        else:
                            P.tt("dve", gsig[:], bk[:, 0:TT], gsig[:], ALU.mult, r=[bkk, "gsig"], w=["gsig"])
                            if br == 1:
                                P.tt("pool", m32[:, m, :], m32[:, m, :], gsig[:], ALU.add, r=[("m32", m), "gsig"], w=[("m32", m)])
                            else:
                                P.tt("pool", mT[:, m, :], m32[:, m, :], gsig[:], ALU.add, r=[("m32", m), "gsig"], w=[("mT", m)])
            mkeys = [("mT", m) for m in range(8)]
            for half in range(2):
                wt, wk = wload(w_out_d[l][:, half * 512:(half + 1) * 512], 512)
                for j in range(NS):
                    bk, bkk = bank()
                    proj_tm(wt, wk, j, 0, 512, bk, bkk, lhs=mT, lkeys=mkeys)
                    P.tt("dve", x_tm[:, j, half * 512:(half + 1) * 512], x_tm[:, j, half * 512:(half + 1) * 512], bk[:, 0:512],
                         ALU.add, r=[("x", j), bkk], w=[("x", j)])
            rmsnorm_to_hT(PB_FNORM, l)
            for cg in range(6):
                ncg = 512 if cg < 5 else 256
                wtg, wkg = wload(w_gu_d[l][:, cg * 512:cg * 512 + ncg], ncg)
                wtu, wku = wload(w_gu_d[l][:, DFF + cg * 512:DFF + cg * 512 + ncg], ncg)
                for c4 in range(ncg // 128):
                    f = cg * 4 + c4
                    bk, bkk = bank()
                    proj_fm(wtg, wkg, c4 * 128, bk, bkk)
                    P.act(sg[:], bk[:, 0:TT], AF.Silu, r=[bkk], w=["gsig"])
                    bk, bkk = bank()
                    proj_fm(wtu, wku, c4 * 128, bk, bkk)
                    P.tt("dve", actT[:, f, :], bk[:, 0:TT], sg[:], ALU.mult, r=[bkk, "gsig"], w=[("actT", f)])
            akeys = [("actT", f) for f in range(22)]
            for half in range(2):
                bks = [bank() for _ in range(NS)]
                for kg in range(3):
                    nk = 8 if kg < 2 else 6
                    wt, wk = wload(w_dn_d[l][kg * 1024:kg * 1024 + nk * 128, half * 512:(half + 1) * 512], 512, kparts=nk)
                    for j in range(NS):
                        bk, bkk = bks[j]
                        for k in range(nk):
                            f = kg * 8 + k
                            P.mm(bk[:, 0:512], actT[:, f, j * 128:(j + 1) * 128], wt[:, k, 0:512], start=(f == 0), stop=(f == 21),
                                 r=[wk] + akeys, w=[bkk])
                for j in range(NS):
                    bk, bkk = bks[j]
                    P.tt("dve", x_tm[:, j, half * 512:(half + 1) * 512], x_tm[:, j, half * 512:(half + 1) * 512], bk[:, 0:512],
                         ALU.add, r=[("x", j), bkk], w=[("x", j)])
        for j in range(NS):
            P.act(junk[:], x_tm[:, j, :], AF.Square, r=[("x", j)], w=["junk", "st"], accum_out=st[:, 0:1], scale=1.0 / 32)
            P.act(st[:, 1:2], st[:, 0:1], AF.Sqrt, r=["st"], w=["st"], bias=EPS)
            P.add("dve", lambda e: e.reciprocal(st[:, 1:2], st[:, 1:2]), r=["st"], w=["st"])
            P.stt("dve", junk[:], x_tm[:, j, :], st[:, 1:2], fin[:], ALU.mult, ALU.mult, r=[("x", j), "st", "fin"], w=["junk"])
            P.dma("sp", y_d[t0 + j * 128:t0 + (j + 1) * 128, :], junk[:], r=["junk"], w=[("y", ti, j)], key="yt")
    P.add("sp", None, r=[("y", ti, j) for ti in range(ntile) for j in range(NS)])
    P.finalize()
    return nc, P


def host_inputs(x, positions, attn_norm, w_in, sgu_ln_g, sgu_ln_b, sgu_w, sgu_b, attn_sinks,
                dn_conv_w, dn_a_log, dn_dt_bias, dn_norm, w_branch, w_out, ffn_norm,
                w_gate_up, w_down, final_norm):
    f32 = np.float32
    B, NT = x.shape[:2]
    par = np.zeros((DEPTH, 128, NPAR), f32)
    for l in range(DEPTH):
        par[l, :, PB_ANORM:PB_ANORM + 8] = np.asarray(attn_norm[l], f32).reshape(8, 128).T
        par[l, :, PB_FNORM:PB_FNORM + 8] = np.asarray(ffn_norm[l], f32).reshape(8, 128).T
        par[l, :, PB_LNG:PB_LNG + 512] = np.asarray(sgu_ln_g[l], f32)[None, :]
        par[l, :, PB_LNB:PB_LNB + 512] = np.asarray(sgu_ln_b[l], f32)[None, :]
        par[l, :, PB_SGUB:PB_SGUB + 512] = np.asarray(sgu_b[l], f32).reshape(1, 512)
        par[l, :, PB_DNN:PB_DNN + 128] = np.asarray(dn_norm[l], f32)[None, :]
        par[l, :, PB_SINK:PB_SINK + 8] = np.asarray(attn_sinks[l], f32)[None, :]
        par[l, :, PB_ALOG:PB_ALOG + 4] = np.asarray(dn_a_log[l], f32)[None, :]
        par[l, :, PB_DTB:PB_DTB + 4] = np.asarray(dn_dt_bias[l], f32)[None, :]
        cw = np.asarray(dn_conv_w[l], f32)
        par[l, :, PB_CONV:PB_CONV + 48] = cw.T.reshape(12, 128, 4).transpose(1, 0, 2).reshape(128, 48)
    fin = np.ascontiguousarray(np.broadcast_to(np.asarray(final_norm, f32)[None, :], (128, D)))
    cst = np.zeros((128, NCONST), f32)
    p = np.arange(128)[:, None]
    f = np.arange(128)[None, :]
    cst[:, CB_ID:CB_ID + 128] = (p == f)
    cst[:, CB_TRIL:CB_TRIL + 128] = (f <= p)
    cst[:, CB_TRIU:CB_TRIU + 128] = (f >= p)
    cst[:, CB_STRL:CB_STRL + 128] = (f < p)
    cst[:, CB_STRU:CB_STRU + 128] = (f > p)
    cst[:, CB_ONES:CB_ONES + 128] = 1.0
    inv_freq = (500000.0 ** (-np.arange(0, 16, 2, dtype=np.float32) / np.float32(16))).astype(f32)
    cst[:, CB_FREQ:CB_FREQ + 8] = inv_freq[None, :]
    shared = dict(par=par, fin=fin, cst=cst, w_in=np.asarray(w_in, f32), sgu_w=np.asarray(sgu_w, f32),
                  w_branch=np.asarray(w_branch, f32), w_out=np.asarray(w_out, f32),
                  w_gate_up=np.asarray(w_gate_up, f32), w_down=np.asarray(w_down, f32))
    per_b = []
    for b in range(B):
        pos = np.asarray(positions[b], np.int32).reshape(NT // 128, 128).T
        d = dict(shared)
        d["x"] = np.ascontiguousarray(np.asarray(x[b], f32))
        d["pos"] = np.ascontiguousarray(pos)
        per_b.append(d)
    return per_b


_CACHE = {}


def kernel(**inputs):
    per_b = host_inputs(**inputs)
    B = len(per_b)
    NT = per_b[0]["x"].shape[0]
    if NT not in _CACHE:
        _CACHE[NT] = build(NT)[0]
    nc = _CACHE[NT]
    n = 8
    in_maps = [per_b[c % B] for c in range(n)]
    res = run_bass_kernel_spmd(nc, in_maps, core_ids=list(range(n)))
    out = np.stack([np.asarray(res.results[b]["y"], np.float32) for b in range(B)], axis=0)
    return out
```

```python
import contextlib
import numpy as np
import concourse.bass as bass
import concourse.mybir as mybir
from concourse.bass_utils import run_bass_kernel_spmd

F32 = mybir.dt.float32
BF16 = mybir.dt.bfloat16
I32 = mybir.dt.int32
AF = mybir.ActivationFunctionType
ALU = mybir.AluOpType

ENGS = ("pe", "act", "dve", "pool", "sp")
EPOCH = 2000

D = 1024
DEPTH = 2
NCOL = 6920
DFF = 2816
EPS = 1e-6
TT = 512
NS = TT // 128


class Op:
    __slots__ = ("eng", "fn", "deps", "sig", "semi", "semv", "dma_key", "dma_val", "waits")

    def __init__(self, eng, fn):
        self.eng = eng
        self.fn = fn
        self.deps = ()
        self.sig = False
        self.semi = -1
        self.semv = 0
        self.dma_key = None
        self.dma_val = 0
        self.waits = None


class Prog:
    def __init__(self, nc):
        self.nc = nc
        self.ops = {e: [] for e in ENGS}
        self.last_w = {}
        self.readers = {}
        self.dma_cnt = {}
        self.stack = contextlib.ExitStack()
        self.ntile = 0
        self.bank_i = 0

    def sb(self, shape, dt, name=None):
        self.ntile += 1
        name = "sb_" + (name or f"t{self.ntile}")
        return self.stack.enter_context(self.nc.sbuf_tensor(name, list(shape), dt))

    def ps(self, shape, dt, name=None):
        self.ntile += 1
        name = name or f"p{self.ntile}"
        return self.stack.enter_context(self.nc.psum_tensor(name, list(shape), dt))

    def add(self, eng, fn, r=(), w=(), dma_key=None):
        op = Op(eng, fn)
        deps = set()
        bk_r = [k for k in r if isinstance(k, tuple) and k[0] == "bank"]
        if bk_r:
            r = [k for k in r if not (isinstance(k, tuple) and k[0] == "bank")]
            w = list(w) + bk_r
        for k in r:
            x = self.last_w.get(k)
            if x is not None:
                deps.add(x)
        for k in w:
            x = self.last_w.get(k)
            if x is not None:
                deps.add(x)
            for y in self.readers.get(k, ()):
                deps.add(y)
        deps.discard(op)
        for k in r:
            self.readers.setdefault(k, []).append(op)
        for k in w:
            self.last_w[k] = op
            self.readers[k] = []
        op.deps = deps
        if dma_key is not None:
            op.dma_key = dma_key
            self.dma_cnt[dma_key] = self.dma_cnt.get(dma_key, 0) + 16
            op.dma_val = self.dma_cnt[dma_key]
        self.ops[eng].append(op)
        return op

    def dma(self, q, out, in_, r=(), w=(), key=None):
        return self.add(q, lambda e: e.dma_start(out=out, in_=in_), r, w, dma_key=key)

    def mm(self, out, lhsT, rhs, start=True, stop=True, r=(), w=()):
        return self.add("pe", lambda e: e.matmul(out, lhsT, rhs, start=start, stop=stop), r, w)

    def act(self, out, in_, func, r=(), w=(), **kw):
        return self.add("act", lambda e: e.activation(out, in_, func, **kw), r, w)

    def tt(self, eng, out, a, b, op, r=(), w=()):
        return self.add(eng, lambda e: e.tensor_tensor(out, a, b, op), r, w)

    def ts(self, eng, out, a, s1, s2, op0, op1=None, r=(), w=()):
        if op1 is None:
            return self.add(eng, lambda e: e.tensor_scalar(out, a, s1, s2, op0), r, w)
        return self.add(eng, lambda e: e.tensor_scalar(out, a, s1, s2, op0, op1), r, w)

    def stt(self, eng, out, a, s, b, op0, op1, r=(), w=()):
        return self.add(eng, lambda e: e.scalar_tensor_tensor(out, a, s, b, op0, op1), r, w)

    def cp(self, eng, out, in_, r=(), w=()):
        if eng == "act":
            return self.add(eng, lambda e: e.copy(out, in_), r, w)
        return self.add(eng, lambda e: e.tensor_copy(out, in_), r, w)

    def memset(self, eng, ap, val, w=()):
        return self.add(eng, lambda e: e.memset(ap, val), (), w)

    def finalize(self):
        nc = self.nc
        for e in ENGS:
            for op in self.ops[e]:
                for d in op.deps:
                    if d.dma_key is None:
                        if d.eng == op.eng and d.eng == "pe":
                            continue
                        d.sig = True
        self.eng_sems = {e: [] for e in ENGS}
        for e in ENGS:
            cnt = 0
            for op in self.ops[e]:
                if op.sig and op.dma_key is None:
                    op.semi = cnt // EPOCH
                    op.semv = cnt % EPOCH + 1
                    cnt += 1
            for i in range((cnt + EPOCH - 1) // EPOCH):
                self.eng_sems[e].append(self.stack.enter_context(nc.semaphore(f"s_{e}_{i}")))
        self.dma_sems = {k: self.stack.enter_context(nc.semaphore(f"d_{i}"))
                         for i, k in enumerate(self.dma_cnt)}
        for e in ENGS:
            seen = {}
            seen_dma = {}
            for op in self.ops[e]:
                need = {}
                need_dma = {}
                for d in op.deps:
                    if d.dma_key is not None:
                        if need_dma.get(d.dma_key, 0) < d.dma_val:
                            need_dma[d.dma_key] = d.dma_val
                    else:
                        if d.eng == e and e == "pe":
                            continue
                        v = (d.semi, d.semv)
                        if need.get(d.eng, (-1, 0)) < v:
                            need[d.eng] = v
                waits = []
                for te, v in need.items():
                    if seen.get(te, (-1, 0)) >= v:
                        continue
                    seen[te] = v
                    waits.append((self.eng_sems[te][v[0]], v[1]))
                for k, v in need_dma.items():
                    if seen_dma.get(k, 0) >= v:
                        continue
                    seen_dma[k] = v
                    waits.append((self.dma_sems[k], v))
                op.waits = waits
        with nc.Block() as block:
            def run(ename):
                def body(eng):
                    for op in self.ops[ename]:
                        for (s, v) in op.waits:
                            eng.wait_ge(s, v)
                        if op.fn is None:
                            continue
                        ins = op.fn(eng)
                        if op.dma_key is not None:
                            ins.then_inc(self.dma_sems[op.dma_key], 16)
                        elif op.sig:
                            ins.then_inc(self.eng_sems[ename][op.semi], 1)
                return body
            block.tensor(run("pe"))
            block.scalar(run("act"))
            block.vector(run("dve"))
            block.gpsimd(run("pool"))
            block.sync(run("sp"))
        self.stack.close()


C_U, C_V, C_QB, C_KB, C_VB, C_QKVC, C_Z, C_BETA, C_A, C_GATE = 0, 512, 1024, 1536, 1664, 1792, 3328, 3840, 3844, 3848

PB_ANORM = 0
PB_FNORM = 8
PB_LNG = 16
PB_LNB = 528
PB_SGUB = 1040
PB_DNN = 1552
PB_SINK = 1680
PB_ALOG = 1688
PB_DTB = 1692
PB_CONV = 1696
NPAR = 1744

CB_ID = 0
CB_TRIL = 128
CB_TRIU = 256
CB_STRL = 384
CB_STRU = 512
CB_ONES = 640
CB_FREQ = 768
NCONST = 776


def build(NT):
    ntile = NT // TT
    nc = bass.Bass("TRN2", target_bir_lowering=False)
    x_d = nc.dram_tensor("x", [NT, D], F32, kind="ExternalInput").ap()
    pos_d = nc.dram_tensor("pos", [128, NT // 128], I32, kind="ExternalInput").ap()
    par_d = nc.dram_tensor("par", [DEPTH, 128, NPAR], F32, kind="ExternalInput").ap()
    fin_d = nc.dram_tensor("fin", [128, D], F32, kind="ExternalInput").ap()
    cst_d = nc.dram_tensor("cst", [128, NCONST], F32, kind="ExternalInput").ap()
    w_in_d = nc.dram_tensor("w_in", [DEPTH, D, NCOL], F32, kind="ExternalInput").ap()
    sguw_d = nc.dram_tensor("sgu_w", [DEPTH, 4, 128, 128], F32, kind="ExternalInput").ap()
    w_br_d = nc.dram_tensor("w_branch", [DEPTH, 3, 512, D], F32, kind="ExternalInput").ap()
    w_out_d = nc.dram_tensor("w_out", [DEPTH, D, D], F32, kind="ExternalInput").ap()
    w_gu_d = nc.dram_tensor("w_gate_up", [DEPTH, D, 2 * DFF], F32, kind="ExternalInput").ap()
    w_dn_d = nc.dram_tensor("w_down", [DEPTH, DFF, D], F32, kind="ExternalInput").ap()
    y_d = nc.dram_tensor("y", [NT, D], F32, kind="ExternalOutput").ap()

    P = Prog(nc)
    cst = P.sb([128, NCONST], F32, "cst")
    par = [P.sb([128, NPAR], F32, f"par{l}") for l in range(DEPTH)]
    idb = P.sb([128, 128], BF16, "idb")
    posf = P.sb([128, NT // 128], F32, "posf")
    posi = P.sb([128, NT // 128], I32, "posi")
    cosT = P.sb([128, NT // 128, 8], F32, "cosT")
    sinT = P.sb([128, NT // 128, 8], F32, "sinT")
    sguWT = [P.sb([128, 4, 128], BF16, f"sguWT{l}") for l in range(DEPTH)]
    esink = [P.sb([128, 8], F32, f"esink{l}") for l in range(DEPTH)]
    negA = [P.sb([128, 4], F32, f"negA{l}") for l in range(DEPTH)]
    S_st = [P.sb([128, 4, 128], F32, f"S{l}") for l in range(DEPTH)]
    kTp = [P.sb([128, 128], BF16, f"kTp{l}") for l in range(DEPTH)]
    vp = [P.sb([128, 2, 65], BF16, f"vp{l}") for l in range(DEPTH)]
    ctail = [P.sb([128, 12, 3], F32, f"ct{l}") for l in range(DEPTH)]
    x_tm = P.sb([128, NS, D], F32, "x_tm")
    hbf = P.sb([128, D], BF16, "hbf")
    hT = P.sb([128, 8, TT], BF16, "hT")
    junk = P.sb([128, D], F32, "junk")
    st = P.sb([128, 16], F32, "st")
    wbuf = [P.sb([128, 8, 512], BF16, f"wbuf{i}") for i in range(3)]
    uT = P.sb([128, 4, TT], BF16, "uT")
    vn = P.sb([128, 512], BF16, "vn")
    oaT = P.sb([128, 4, TT], BF16, "oaT")
    qkvb = P.sb([128, 768], F32, "qkvb")
    rtmp = P.sb([128, 10, 8], F32, "rtmp")
    qTb = P.sb([128, 4, 128], BF16, "qTb")
    kTb = P.sb([128, 128], BF16, "kTb")
    vb = P.sb([128, 2, 65], BF16, "vb")
    pT = P.sb([128, 2, 128], BF16, "pTs")
    obn = P.sb([128, 8, 65], F32, "obn")
    ob = P.sb([128, 512], BF16, "ob")
    obT = P.sb([128, 4, TT], BF16, "obT")
    qkvc = P.sb([128, 3 + TT], F32, "qkvc")
    cacc = P.sb([128, TT], F32, "cacc")
    qkvs = P.sb([128, 12, TT], F32, "qkvs")
    zba = P.sb([128, NS, 520], F32, "zba")
    gt = P.sb([128, 16], F32, "gt")
    vtm = cacc
    pf = P.sb([128, 2, 128], F32, "pf")
    dn = {"tmp": P.sb([128, 128], F32, "dn_tmp")}
    dnh = []
    for h in range(2):
        d_ = {n: P.sb([128, 128], F32, f"dn{h}_" + n) for n in
              ["vtm", "dec", "decT", "o", "kdec", "attT", "vnew", "TT", "L", "M", "X", "XT"]}
        dnh.append(d_)
    dsm = P.sb([128, 4, 8], F32, "dsm")
    ocb = P.sb([128, 512], BF16, "ocb")
    ocT = P.sb([128, 4, TT], BF16, "ocT")
    gsig = P.sb([128, TT], F32, "gsig")
    sg = gsig
    m32 = P.sb([128, 8, TT], F32, "m32")
    mT = P.sb([128, 8, TT], BF16, "mT")
    actT = P.sb([128, 22, TT], BF16, "actT")
    banks = [P.ps([128, 512], F32, f"bank{i}") for i in range(8)]

    def bank():
        i = P.bank_i
        P.bank_i = (i + 1) % 8
        return banks[i], ("bank", i)

    wrot = [0]

    def wload(src_ap, ncols, kparts=8):
        i = wrot[0]
        wrot[0] = (i + 1) % 3
        t = wbuf[i]
        P.dma("pool", t[:, 0:kparts, 0:ncols], src_ap.rearrange("(k p) c -> p k c", p=128),
              w=[("wbuf", i)], key=("wbuf", i))
        return t, ("wbuf", i)

    ident = cst[:, CB_ID:CB_ID + 128]

    P.dma("sp", cst[:], cst_d[:, :], w=["cst"], key="cst")
    for l in range(DEPTH):
        P.dma("sp", par[l][:], par_d[l], w=[("par", l)], key=("par", l))
    P.dma("sp", posi[:], pos_d[:, :], w=["posi"], key="posi")
    P.cp("dve", idb[:], ident, r=["cst"], w=["idb"])
    P.cp("dve", posf[:], posi[:], r=["posi"], w=["posf"])
    TWO_PI = 2.0 * np.pi
    nsub = NT // 128
    for which, shift, dst in (("s", np.pi, sinT), ("c", 1.5 * np.pi, cosT)):
        for j in range(nsub):
            a = rtmp[:, 0, :]
            k_f = rtmp[:, 1, :]
            k_i = rtmp[:, 2, :].bitcast(I32)
            fix = rtmp[:, 3, :]
            P.ts("dve", a, cst[:, CB_FREQ:CB_FREQ + 8], posf[:, j:j + 1], float(shift), ALU.mult, ALU.add,
                 r=["cst", "posf"], w=["rtmp"])
            P.ts("dve", k_f, a, 1.0 / TWO_PI, None, ALU.mult, r=["rtmp"], w=["rtmp"])
            P.cp("dve", k_i, k_f, r=["rtmp"], w=["rtmp"])
            P.cp("dve", k_f, k_i, r=["rtmp"], w=["rtmp"])
            P.stt("dve", a, k_f, -TWO_PI, a, ALU.mult, ALU.add, r=["rtmp"], w=["rtmp"])
            P.ts("dve", fix, a, 0.0, TWO_PI, ALU.is_lt, ALU.mult, r=["rtmp"], w=["rtmp"])
            P.tt("dve", a, a, fix, ALU.add, r=["rtmp"], w=["rtmp"])
            P.ts("dve", fix, a, TWO_PI, -TWO_PI, ALU.is_ge, ALU.mult, r=["rtmp"], w=["rtmp"])
            P.tt("dve", a, a, fix, ALU.add, r=["rtmp"], w=["rtmp"])
            P.ts("dve", a, a, -np.pi, None, ALU.add, r=["rtmp"], w=["rtmp"])
            P.ts("dve", a, a, -3.1415925, 3.1415925, ALU.max, ALU.min, r=["rtmp"], w=["rtmp"])
            P.act(dst[:, j, :], a, AF.Sin, r=["rtmp"], w=[("rope", which)])
    for l in range(DEPTH):
        wtmp = dn["tmp"]
        for g in range(4):
            P.dma("sp", wtmp[:], sguw_d[l, g], w=["dn_tmp"], key="dn_tmp")
            P.tt("dve", wtmp[:], wtmp[:], cst[:, CB_TRIL:CB_TRIL + 128], ALU.mult, r=["dn_tmp", "cst"], w=["dn_tmp"])
            bk, bkk = bank()
            P.mm(bk[:, 0:128], wtmp[:], ident, r=["dn_tmp", "cst"], w=[bkk])
            P.cp("dve", sguWT[l][:, g, :], bk[:, 0:128], r=[bkk], w=[("sguWT", l)])
        P.act(esink[l][:], par[l][:, PB_SINK:PB_SINK + 8], AF.Exp, r=[("par", l)], w=[("esink", l)])
        P.act(negA[l][:], par[l][:, PB_ALOG:PB_ALOG + 4], AF.Exp, r=[("par", l)], w=[("negA", l)])
        P.ts("dve", negA[l][:], negA[l][:], -1.0, None, ALU.mult, r=[("negA", l)], w=[("negA", l)])
        P.memset("dve", S_st[l][:], 0.0, w=[("S", l, h) for h in range(4)])
        P.memset("dve", ctail[l][:], 0.0, w=[("ctail", l)])
        P.memset("dve", kTp[l][:], 0.0, w=[("kTp", l)])
        P.memset("dve", vp[l][:], 0.0, w=[("vp", l)])

    def rmsnorm_to_hT(gcol, l):
        for j in range(NS):
            P.act(junk[:], x_tm[:, j, :], AF.Square, r=[("x", j)], w=["junk", "st"], accum_out=st[:, 0:1], scale=1.0 / 32)
            P.act(st[:, 1:2], st[:, 0:1], AF.Sqrt, r=["st"], w=["st"], bias=EPS)
            P.add("dve", lambda e: e.reciprocal(st[:, 1:2], st[:, 1:2]), r=["st"], w=["st"])
            P.ts("dve", hbf[:], x_tm[:, j, :], st[:, 1:2], None, ALU.mult, r=[("x", j), "st"], w=["hbf"])
            for half in range(2):
                bk, bkk = bank()
                for q in range(4):
                    k = half * 4 + q
                    P.mm(bk[:, q * 128:(q + 1) * 128], hbf[:, k * 128:(k + 1) * 128], idb[:], r=["hbf", "idb"], w=[bkk])
                for q in range(4):
                    k = half * 4 + q
                    eng = "dve" if q % 2 == 0 else "pool"
                    eng = "dve"
                    P.ts(eng, hT[:, k, j * 128:(j + 1) * 128], bk[:, q * 128:(q + 1) * 128],
                         par[l][:, gcol + k:gcol + k + 1], None, ALU.mult, r=[bkk, ("par", l)], w=[("hT", j)])

    hT_all = [("hT", j) for j in range(NS)]

    def proj_fm(wt, wk, c, bk, bkk, ncol=128, kparts=8, rhs=None, rkeys=None):
        rhs = hT if rhs is None else rhs
        rkeys = hT_all if rkeys is None else rkeys
        for k in range(kparts):
            P.mm(bk[0:ncol, 0:TT], wt[:, k, c:c + ncol], rhs[:, k, :], start=(k == 0), stop=(k == kparts - 1),
                 r=[wk] + rkeys, w=[bkk])

    def proj_tm(wt, wk, j, c, n, bk, bkk, kparts=8, lhs=None, lkeys=None):
        lhs = hT if lhs is None else lhs
        lkeys = [("hT", j)] if lkeys is None else lkeys
        for k in range(kparts):
            P.mm(bk[:, 0:n], lhs[:, k, j * 128:(j + 1) * 128], wt[:, k, c:c + n], start=(k == 0), stop=(k == kparts - 1),
                 r=[wk] + lkeys, w=[bkk])

    def transpose_to(dst_fn, src, nblk, skeys, dkeys):
        bk, bkk = bank()
        for q in range(nblk):
            P.mm(bk[:, q * 128:(q + 1) * 128], src[:, q * 128:(q + 1) * 128], idb[:], r=skeys + ["idb"], w=[bkk])
        for q in range(nblk):
            P.cp("act" if q % 2 else "dve", dst_fn(q), bk[:, q * 128:(q + 1) * 128], r=[bkk], w=dkeys)

    for ti in range(ntile):
        t0 = ti * TT
        for j in range(NS):
            P.dma("sp", x_tm[:, j, :], x_d[t0 + j * 128:t0 + (j + 1) * 128, :], w=[("x", j)], key=("x", j))
        for l in range(DEPTH):
            pl = par[l]
            pk = ("par", l)
            w_in_l = w_in_d[l]
            rmsnorm_to_hT(PB_ANORM, l)
            wt, wk = wload(w_in_l[:, C_U:C_U + 512], 512)
            for c in range(4):
                bk, bkk = bank()
                proj_fm(wt, wk, c * 128, bk, bkk)
                P.act(uT[:, c, :], bk[:, 0:TT], AF.Gelu_apprx_tanh, r=[bkk], w=[("uT", c)])
            wt, wk = wload(w_in_l[:, C_V:C_V + 512], 512)
            for j in range(NS):
                bk, bkk = bank()
                proj_tm(wt, wk, j, 0, 512, bk, bkk)
                P.act(vtm[:], bk[:, 0:512], AF.Gelu_apprx_tanh, r=[bkk], w=["cacc", "st"], accum_out=st[:, 2:3])
                P.act(junk[:, 0:512], vtm[:], AF.Square, r=["cacc"], w=["junk", "st"], accum_out=st[:, 3:4])
                P.ts("dve", st[:, 4:5], st[:, 2:3], 1.0 / 512, None, ALU.mult, r=["st"], w=["st"])
                P.tt("dve", st[:, 5:6], st[:, 4:5], st[:, 4:5], ALU.mult, r=["st"], w=["st"])
                P.stt("dve", st[:, 6:7], st[:, 3:4], 1.0 / 512, st[:, 5:6], ALU.mult, ALU.subtract, r=["st"], w=["st"])
                P.act(st[:, 6:7], st[:, 6:7], AF.Sqrt, r=["st"], w=["st"], bias=EPS)
                P.add("dve", lambda e: e.reciprocal(st[:, 6:7], st[:, 6:7]), r=["st"], w=["st"])
                P.stt("dve", st[:, 7:8], st[:, 4:5], -1.0, st[:, 6:7], ALU.mult, ALU.mult, r=["st"], w=["st"])
                P.act(vtm[:], vtm[:], AF.Identity, r=["cacc", "st"], w=["cacc"], scale=st[:, 6:7], bias=st[:, 7:8])
                P.tt("dve", vtm[:], vtm[:], pl[:, PB_LNG:PB_LNG + 512], ALU.mult, r=["cacc", pk], w=["cacc"])
                P.tt("dve", vn[:], vtm[:], pl[:, PB_LNB:PB_LNB + 512], ALU.add, r=["cacc", pk], w=["vn"])
                bk, bkk = bank()
                for g in range(4):
                    P.mm(bk[:, g * 128:(g + 1) * 128], vn[:, g * 128:(g + 1) * 128], sguWT[l][:, g, :],
                         r=["vn", ("sguWT", l)], w=[bkk])
                for g in range(4):
                    P.tt("dve", junk[:, 0:128], bk[:, g * 128:(g + 1) * 128], pl[:, PB_SGUB + g * 128:PB_SGUB + (g + 1) * 128],
                         ALU.add, r=[bkk, pk], w=["junk"])
                    P.tt("dve", oaT[:, g, j * 128:(j + 1) * 128], junk[:, 0:128], uT[:, g, j * 128:(j + 1) * 128], ALU.mult,
                         r=["junk", ("uT", g)], w=[("oaT", g)])
            wt, wk = wload(w_in_l[:, C_QB:C_QB + 512], 512)
            wt2, wk2 = wload(w_in_l[:, C_KB:C_KB + 256], 256)
            for j in range(NS):
                gj = ti * NS + j
                bk, bkk = bank()
                proj_tm(wt, wk, j, 0, 512, bk, bkk)
                P.cp("act", qkvb[:, 0:512], bk[:, 0:512], r=[bkk], w=["qkvb"])
                bk, bkk = bank()
                proj_tm(wt2, wk2, j, 0, 256, bk, bkk)
                P.cp("act", qkvb[:, 512:768], bk[:, 0:256], r=[bkk], w=["qkvb"])
                qk = qkvb[:, 0:640].rearrange("p (h d) -> p h d", d=64)
                x1 = qk[:, :, 0:8]
                x2 = qk[:, :, 8:16]
                cb = cosT[:, gj:gj + 1, :].to_broadcast([128, 10, 8])
                sb_ = sinT[:, gj:gj + 1, :].to_broadcast([128, 10, 8])
                rk = ["qkvb", ("rope", "s"), ("rope", "c")]
                t1 = rtmp[:, :, :]
                P.tt("dve", t1, x1, sb_, ALU.mult, r=rk, w=["rtmp"])
                P.tt("dve", x1, x1, cb, ALU.mult, r=rk, w=["qkvb"])
                t2 = junk[:, 0:80].rearrange("p (h d) -> p h d", d=8)
                P.tt("dve", t2, x2, sb_, ALU.mult, r=rk, w=["junk"])
                P.tt("dve", x1, x1, t2, ALU.subtract, r=["qkvb", "junk"], w=["qkvb"])
                P.tt("dve", x2, x2, cb, ALU.mult, r=rk, w=["qkvb"])
                P.tt("dve", x2, x2, t1, ALU.add, r=["qkvb", "rtmp"], w=["qkvb"])
                qb16 = ob[:, :]
                for c in range(4):
                    P.cp("dve", qb16[:, c * 128:c * 128 + 64], qkvb[:, c * 64:(c + 1) * 64], r=["qkvb"], w=["ob"])
                    P.cp("dve", qb16[:, c * 128 + 64:(c + 1) * 128], qkvb[:, (c + 4) * 64:(c + 5) * 64], r=["qkvb"], w=["ob"])
                kb16 = ocb[:, 0:128]
                P.cp("dve", kb16, qkvb[:, 512:640], r=["qkvb"], w=["ocb"])
                P.memset("dve", vb[:, :, 64:65], 1.0, w=["vb"])
                P.cp("dve", vb[:, :, 0:64], qkvb[:, 640:768].rearrange("p (h d) -> p h d", d=64), r=["qkvb"], w=["vb"])
                transpose_to(lambda q: qTb[:, q, :], qb16, 4, ["ob"], ["qTb"])
                transpose_to(lambda q: kTb[:, :], kb16, 1, ["ocb"], ["kTb"])
                for h in range(8):
                    kv = h // 4
                    c = h % 4
                    pb = kv * 64
                    bk, bkk = bank()
                    P.mm(bk[:, 0:128], kTb[pb:pb + 64, :], qTb[pb:pb + 64, c, :], r=["kTb", "qTb"], w=[bkk])
                    nb = 1
                    if gj > 0:
                        P.mm(bk[:, 128:256], kTp[l][pb:pb + 64, :], qTb[pb:pb + 64, c, :], r=[("kTp", l), "qTb"], w=[bkk])
                        nb = 2
                    P.act(pf[:, 0:nb, :], bk[:, 0:nb * 128].rearrange("p (b q) -> p b q", b=nb), AF.Exp, r=[bkk], w=["pf"], scale=0.125)
                    P.tt("dve", pT[:, 0, :], pf[:, 0, :], cst[:, CB_TRIU:CB_TRIU + 128], ALU.mult, r=["pf", "cst"], w=["pT"])
                    if nb == 2:
                        P.tt("dve", pT[:, 1, :], pf[:, 1, :], cst[:, CB_STRL:CB_STRL + 128], ALU.mult, r=["pf", "cst"], w=["pT"])
                    bk2, bkk2 = bank()
                    P.mm(bk2[:, 0:65], pT[:, 0, :], vb[:, kv, :], start=True, stop=(nb == 1), r=["pT", "vb"], w=[bkk2])
                    if nb == 2:
                        P.mm(bk2[:, 0:65], pT[:, 1, :], vp[l][:, kv, :], start=False, stop=True, r=["pT", ("vp", l)], w=[bkk2])
                    P.cp("act", obn[:, h, :], bk2[:, 0:65], r=[bkk2], w=["obn"])
                P.tt("dve", st[:, 8:16], obn[:, :, 64], esink[l][:], ALU.add, r=["obn", ("esink", l)], w=["st"])
                P.add("dve", lambda e: e.reciprocal(st[:, 8:16], st[:, 8:16]), r=["st"], w=["st"])
                P.tt("dve", ob[:, :].rearrange("p (h d) -> p h d", d=64), obn[:, :, 0:64],
                     st[:, 8:16].unsqueeze(2).to_broadcast([128, 8, 64]), ALU.mult, r=["obn", "st"], w=["ob"])
                transpose_to(lambda q, j=j: obT[:, q, j * 128:(j + 1) * 128], ob, 4, ["ob"], [("obT", j)])
                P.cp("dve", kTp[l][:], kTb[:], r=["kTb"], w=[("kTp", l)])
                P.cp("dve", vp[l][:], vb[:], r=["vb"], w=[("vp", l)])
            for cg in range(3):
                wt, wk = wload(w_in_l[:, C_QKVC + cg * 512:C_QKVC + (cg + 1) * 512], 512)
                for c4 in range(4):
                    c = cg * 4 + c4
                    bk, bkk = bank()
                    proj_fm(wt, wk, c4 * 128, bk, bkk)
                    P.cp("dve", qkvc[:, 0:3], ctail[l][:, c, :], r=[("ctail", l)], w=["qkvc"])
                    P.cp("act", qkvc[:, 3:3 + TT], bk[:, 0:TT], r=[bkk], w=["qkvc"])
                    P.cp("dve", ctail[l][:, c, :], qkvc[:, TT:TT + 3], r=["qkvc"], w=[("ctail", l)])
                    cw = PB_CONV + c * 4
                    P.ts("dve", cacc[:], qkvc[:, 0:TT], pl[:, cw:cw + 1], None, ALU.mult, r=["qkvc", pk], w=["cacc"])
                    for i in range(1, 4):
                        P.stt("dve", cacc[:], qkvc[:, i:i + TT], pl[:, cw + i:cw + i + 1], cacc[:], ALU.mult, ALU.add,
                              r=["qkvc", pk, "cacc"], w=["cacc"])
                    P.act(qkvs[:, c, :], cacc[:], AF.Silu, r=["cacc"], w=[("qkvs", c)])
                    if c < 8:
                        P.act(cacc[:], qkvs[:, c, :], AF.Square, r=[("qkvs", c)], w=["cacc"])
                        bk, bkk = bank()
                        P.mm(bk[:, 0:TT], cst[:, CB_ONES:CB_ONES + 128], cacc[:], r=["cst", "cacc"], w=[bkk])
                        P.act(cacc[:], bk[:, 0:TT], AF.Sqrt, r=[bkk], w=["cacc"], bias=EPS)
                        P.add("dve", lambda e: e.reciprocal(cacc[:], cacc[:]), r=["cacc"], w=["cacc"])
                        if c < 4:
                            P.stt("dve", qkvs[:, c, :], qkvs[:, c, :], float(128 ** -0.5), cacc[:], ALU.mult, ALU.mult,
                                  r=[("qkvs", c), "cacc"], w=[("qkvs", c)])
                        else:
                            P.tt("dve", qkvs[:, c, :], qkvs[:, c, :], cacc[:], ALU.mult, r=[("qkvs", c), "cacc"], w=[("qkvs", c)])
            wt, wk = wload(w_in_l[:, C_Z:C_Z + 512], 512)
            wt2, wk2 = wload(w_in_l[:, C_BETA:C_BETA + 8], 8)
            for j in range(NS):
                bk, bkk = bank()
                proj_tm(wt, wk, j, 0, 512, bk, bkk)
                P.act(zba[:, j, 0:512], bk[:, 0:512], AF.Silu, r=[bkk], w=[("zba", j)])
                bk, bkk = bank()
                proj_tm(wt2, wk2, j, 0, 8, bk, bkk)
                P.cp("act", zba[:, j, 512:520], bk[:, 0:8], r=[bkk], w=[("zba", j)])
            for j in range(NS):
                tsl = slice(j * 128, (j + 1) * 128)
                P.act(gt[:, 0:4], zba[:, j, 512:516], AF.Sigmoid, r=[("zba", j)], w=["gt"])
                P.tt("dve", gt[:, 4:8], zba[:, j, 516:520], pl[:, PB_DTB:PB_DTB + 4], ALU.add, r=[("zba", j), pk], w=["gt"])
                P.act(gt[:, 4:8], gt[:, 4:8], AF.Exp, r=["gt"], w=["gt"])
                P.act(gt[:, 4:8], gt[:, 4:8], AF.Ln, r=["gt"], w=["gt"], bias=1.0)
                P.tt("dve", gt[:, 4:8], gt[:, 4:8], negA[l][:], ALU.mult, r=["gt", ("negA", l)], w=["gt"])
                bk, bkk = bank()
                P.mm(bk[:, 0:4], cst[:, CB_TRIU:CB_TRIU + 128], gt[:, 4:8], r=["cst", "gt"], w=[bkk])
                P.cp("dve", gt[:, 8:12], bk[:, 0:4], r=[bkk], w=["gt"])
                P.act(gt[:, 12:16], gt[:, 8:12], AF.Exp, r=["gt"], w=["gt"])
                def head_gen(h, j=j, tsl=tsl):
                    qT_h = qkvs[:, h, tsl]
                    kT_h = qkvs[:, 4 + h, tsl]
                    vT_h = qkvs[:, 8 + h, tsl]
                    kq = [("qkvs", h), ("qkvs", 4 + h), ("qkvs", 8 + h)]
                    beta = gt[:, h:h + 1]
                    gc = gt[:, 8 + h:9 + h]
                    egc = gt[:, 12 + h:13 + h]
                    T = dnh[h % 2]
                    K = lambda n: f"dn_{n}_{h % 2}"
                    ds = dsm[:, h, :]
                    dk = ("dsm", h)
                    bk, bkk = bank()
                    P.mm(bk[:, 0:128], kT_h, ident, r=kq + ["cst"], w=[bkk])
                    P.mm(bk[:, 128:256], vT_h, ident, r=kq + ["cst"], w=[bkk])
                    P.ts("dve", T["decT"][:], cst[:, CB_TRIU:CB_TRIU + 128], gt[:, 4 + h:5 + h], None, ALU.mult,
                         r=["cst", "gt"], w=[K("decT")])
                    bkb, bkkb = bank()
                    P.mm(bkb[:, 0:128], cst[:, CB_ONES:CB_ONES + 128], T["decT"][:], r=["cst", K("decT")], w=[bkkb])
                    yield
                    P.cp("act", T["vtm"][:], bk[:, 128:256], r=[bkk], w=[K("vtm")])
                    P.act(ds[:, 0:1], bkb[:, 127:128], AF.Exp, r=[bkkb], w=[dk])
                    P.ts("dve", ds[:, 1:2], bkb[:, 127:128], gc, None, ALU.subtract, r=[bkkb, "gt"], w=[dk])
                    P.act(ds[:, 1:2], ds[:, 1:2], AF.Exp, r=[dk], w=[dk])
                    P.ts("dve", ds[:, 2:3], egc, -1.0, None, ALU.mult, r=["gt"], w=[dk])
                    P.ts("dve", T["kdec"][:], bk[:, 0:128], ds[:, 1:2], None, ALU.mult, r=[bkk, dk], w=[K("kdec")])
                    P.ts("dve", T["dec"][:], bkb[:, 0:128], gc, 0.0, ALU.subtract, ALU.max, r=[bkkb, "gt"], w=[K("dec")])
                    P.ts("dve", T["decT"][:], bkb[:, 0:128], gc, 0.0, ALU.subtract, ALU.min, r=[bkkb, "gt"], w=[K("decT")])
                    P.act(T["dec"][:], T["dec"][:], AF.Exp, r=[K("dec")], w=[K("dec")], scale=-1.0)
                    P.act(T["decT"][:], T["decT"][:], AF.Exp, r=[K("decT")], w=[K("decT")])
                    bk, bkk = bank()
                    P.mm(bk[:, 0:128], kT_h, kT_h, r=kq, w=[bkk])
                    P.mm(bk[:, 128:256], kT_h, qT_h, r=kq, w=[bkk])
                    yield
                    P.tt("dve", T["dec"][:], T["dec"][:], cst[:, CB_STRL:CB_STRL + 128], ALU.mult, r=[K("dec"), "cst"], w=[K("dec")])
                    P.stt("dve", T["L"][:], bk[:, 0:128], beta, T["dec"][:], ALU.mult, ALU.mult, r=[bkk, "gt", K("dec")], w=[K("L")])
                    P.tt("dve", T["decT"][:], T["decT"][:], cst[:, CB_TRIU:CB_TRIU + 128], ALU.mult, r=[K("decT"), "cst"], w=[K("decT")])
                    P.tt("dve", T["attT"][:], bk[:, 128:256], T["decT"][:], ALU.mult, r=[bkk, K("decT")], w=[K("attT")])
                    bk, bkk = bank()
                    P.mm(bk[:, 0:128], T["L"][:], ident, r=[K("L"), "cst"], w=[bkk])
                    yield
                    P.cp("act", T["XT"][:], bk[:, 0:128], r=[bkk], w=[K("XT")])
                    P.tt("dve", T["TT"][:], ident, bk[:, 0:128], ALU.subtract, r=["cst", bkk], w=[K("TT")])
                    X, XT = T["L"], T["XT"]
                    xk, xtk = K("L"), K("XT")
                    X2, XT2 = T["X"], T["M"]
                    x2k, xt2k = K("X"), K("M")
                    for step in range(6):
                        bk, bkk = bank()
                        P.mm(bk[:, 0:128], XT[:], X[:], r=[xk, xtk], w=[bkk])
                        if step < 5:
                            P.mm(bk[:, 128:256], X[:], XT[:], r=[xk, xtk], w=[bkk])
                        yield
                        P.cp("act", X2[:], bk[:, 0:128], r=[bkk], w=[x2k])
                        if step < 5:
                            P.cp("dve", XT2[:], bk[:, 128:256], r=[bkk], w=[xt2k])
                        bk3, bkk3 = bank()
                        P.mm(bk3[:, 0:128], X2[:], T["TT"][:], r=[x2k, K("TT")], w=[bkk3])
                        yield
                        P.tt("dve", T["TT"][:], T["TT"][:], bk3[:, 0:128], ALU.add, r=[K("TT"), bkk3], w=[K("TT")])
                        X, XT, X2, XT2 = X2, XT2, X, XT
                        xk, xtk, x2k, xt2k = x2k, xt2k, xk, xtk
                    Sh = S_st[l][:, h, :]
                    Shb = Sh
                    sk = ("S", l, h)
                    bk, bkk = bank()
                    P.mm(bk[:, 0:128], kT_h, Shb, r=kq + [sk], w=[bkk])
                    P.mm(bk[:, 128:256], qT_h, Shb, r=kq + [sk], w=[bkk])
                    yield
                    P.stt("dve", T["o"][:], bk[:, 0:128], ds[:, 2:3], T["vtm"][:], ALU.mult, ALU.add,
                          r=[bkk, dk, K("vtm")], w=[K("o")])
                    P.ts("dve", T["vtm"][:], T["o"][:], beta, None, ALU.mult, r=[K("o"), "gt"], w=[K("vtm")])
                    P.ts("dve", T["o"][:], bk[:, 128:256], egc, None, ALU.mult, r=[bkk, "gt"], w=[K("o")])
                    bk, bkk = bank()
                    P.mm(bk[:, 0:128], T["TT"][:], T["vtm"][:], r=[K("TT"), K("vtm")], w=[bkk])
                    yield
                    P.cp("act", T["vnew"][:], bk[:, 0:128], r=[bkk], w=[K("vnew")])
                    bk, bkk = bank()
                    P.mm(bk[:, 0:128], T["attT"][:], T["vnew"][:], r=[K("attT"), K("vnew")], w=[bkk])
                    P.mm(bk[:, 128:256], T["kdec"][:], T["vnew"][:], r=[K("kdec"), K("vnew")], w=[bkk])
                    yield
                    P.tt("dve", T["o"][:], T["o"][:], bk[:, 0:128], ALU.add, r=[K("o"), bkk], w=[K("o")])
                    P.stt("dve", Sh, Sh, ds[:, 0:1], bk[:, 128:256], ALU.mult, ALU.add, r=[sk, dk, bkk], w=[sk])
                    P.act(T["dec"][:], T["o"][:], AF.Square, r=[K("o")], w=[K("dec"), dk], accum_out=ds[:, 3:4],
                          scale=float(128 ** -0.5))
                    P.act(ds[:, 4:5], ds[:, 3:4], AF.Sqrt, r=[dk], w=[dk], bias=EPS)
                    P.add("dve", lambda e: e.reciprocal(ds[:, 4:5], ds[:, 4:5]), r=[dk], w=[dk])
                    P.stt("dve", T["o"][:], T["o"][:], ds[:, 4:5], pl[:, PB_DNN:PB_DNN + 128], ALU.mult, ALU.mult,
                          r=[K("o"), dk, pk], w=[K("o")])
                    P.tt("dve", ocb[:, h * 128:(h + 1) * 128], T["o"][:], zba[:, j, h * 128:(h + 1) * 128], ALU.mult,
                         r=[K("o"), ("zba", j)], w=["ocb"])

                for pair in ((0, 1), (2, 3)):
                    live = [head_gen(h) for h in pair]
                    while live:
                        for g in list(live):
                            try:
                                next(g)
                            except StopIteration:
                                live.remove(g)

                transpose_to(lambda q, j=j: ocT[:, q, j * 128:(j + 1) * 128], ocb, 4, ["ocb"], [("ocT", j)])
            srcs = [(oaT, [("oaT", g) for g in range(4)]), (obT, [("obT", j) for j in range(NS)]),
                    (ocT, [("ocT", j) for j in range(NS)])]
            for br in range(3):
                srcT, skeys = srcs[br]
                for half in range(2):
                    wtb, wkb = wload(w_br_d[l, br][:, half * 512:(half + 1) * 512], 512, kparts=4)
                    wtg, wkg = wload(w_in_l[:, C_GATE + br * 1024 + half * 512:C_GATE + br * 1024 + (half + 1) * 512], 512)
                    for c4 in range(4):
                        m = half * 4 + c4
                        bk, bkk = bank()
                        proj_fm(wtg, wkg, c4 * 128, bk, bkk)
                        P.act(gsig[:], bk[:, 0:TT], AF.Sigmoid, r=[bkk], w=["gsig"])
                        bk, bkk = bank()
                        proj_fm(wtb, wkb, c4 * 128, bk, bkk, kparts=4, rhs=srcT, rkeys=skeys)
                        if br == 0:
                            P.tt("dve", m32[:, m, :], bk[:, 0:TT], gsig[:], ALU.mult, r=[bkk, "gsig"], w=[("m32", m)])
                        else:
                            P.tt("dve", gsig[:], bk[:, 0:TT], gsig[:], ALU.mult, r=[bkk, "gsig"], w=["gsig"])
                            if br == 1:
                                P.tt("pool", m32[:, m, :], m32[:, m, :], gsig[:], ALU.add, r=[("m32", m), "gsig"], w=[("m32", m)])
                            else:
                                P.tt("pool", mT[:, m, :], m32[:, m, :], gsig[:], ALU.add, r=[("m32", m), "gsig"], w=[("mT", m)])
            mkeys = [("mT", m) for m in range(8)]
            for half in range(2):
                wt, wk = wload(w_out_d[l][:, half * 512:(half + 1) * 512], 512)
                for j in range(NS):
                    bk, bkk = bank()
                    proj_tm(wt, wk, j, 0, 512, bk, bkk, lhs=mT, lkeys=mkeys)
                    P.tt("dve", x_tm[:, j, half * 512:(half + 1) * 512], x_tm[:, j, half * 512:(half + 1) * 512], bk[:, 0:512],
                         ALU.add, r=[("x", j), bkk], w=[("x", j)])
            rmsnorm_to_hT(PB_FNORM, l)
            for cg in range(6):
                ncg = 512 if cg < 5 else 256
                wtg, wkg = wload(w_gu_d[l][:, cg * 512:cg * 512 + ncg], ncg)
                wtu, wku = wload(w_gu_d[l][:, DFF + cg * 512:DFF + cg * 512 + ncg], ncg)
                for c4 in range(ncg // 128):
                    f = cg * 4 + c4
                    bk, bkk = bank()
                    proj_fm(wtg, wkg, c4 * 128, bk, bkk)
                    P.act(sg[:], bk[:, 0:TT], AF.Silu, r=[bkk], w=["gsig"])
                    bk, bkk = bank()
                    proj_fm(wtu, wku, c4 * 128, bk, bkk)
                    P.tt("dve", actT[:, f, :], bk[:, 0:TT], sg[:], ALU.mult, r=[bkk, "gsig"], w=[("actT", f)])
            akeys = [("actT", f) for f in range(22)]
            for half in range(2):
                bks = [bank() for _ in range(NS)]
                for kg in range(3):
                    nk = 8 if kg < 2 else 6
                    wt, wk = wload(w_dn_d[l][kg * 1024:kg * 1024 + nk * 128, half * 512:(half + 1) * 512], 512, kparts=nk)
                    for j in range(NS):
                        bk, bkk = bks[j]
                        for k in range(nk):
                            f = kg * 8 + k
                            P.mm(bk[:, 0:512], actT[:, f, j * 128:(j + 1) * 128], wt[:, k, 0:512], start=(f == 0), stop=(f == 21),
                                 r=[wk] + akeys, w=[bkk])
                for j in range(NS):
                    bk, bkk = bks[j]
                    P.tt("dve", x_tm[:, j, half * 512:(half + 1) * 512], x_tm[:, j, half * 512:(half + 1) * 512], bk[:, 0:512],
                         ALU.add, r=[("x", j), bkk], w=[("x", j)])
        fin = m32[:, 0:2, :].rearrange("p a b -> p (a b)")
        P.dma("sp", fin, fin_d[:, :], w=[("m32", 0), ("m32", 1)], key="fin")
        for j in range(NS):
            P.act(junk[:], x_tm[:, j, :], AF.Square, r=[("x", j)], w=["junk", "st"], accum_out=st[:, 0:1], scale=1.0 / 32)
            P.act(st[:, 1:2], st[:, 0:1], AF.Sqrt, r=["st"], w=["st"], bias=EPS)
            P.add("dve", lambda e: e.reciprocal(st[:, 1:2], st[:, 1:2]), r=["st"], w=["st"])
            P.stt("dve", junk[:], x_tm[:, j, :], st[:, 1:2], fin[:], ALU.mult, ALU.mult, r=[("x", j), "st", ("m32", 0), ("m32", 1)], w=["junk"])
            P.dma("sp", y_d[t0 + j * 128:t0 + (j + 1) * 128, :], junk[:], r=["junk"], w=[("y", ti, j)], key="yt")
    P.add("sp", None, r=[("y", ti, j) for ti in range(ntile) for j in range(NS)])
    P.finalize()
    return nc, P


def host_inputs(x, positions, attn_norm, w_in, sgu_ln_g, sgu_ln_b, sgu_w, sgu_b, attn_sinks,
                dn_conv_w, dn_a_log, dn_dt_bias, dn_norm, w_branch, w_out, ffn_norm,
                w_gate_up, w_down, final_norm):
    f32 = np.float32
    B, NT = x.shape[:2]
    par = np.zeros((DEPTH, 128, NPAR), f32)
    for l in range(DEPTH):
        par[l, :, PB_ANORM:PB_ANORM + 8] = np.asarray(attn_norm[l], f32).reshape(8, 128).T
        par[l, :, PB_FNORM:PB_FNORM + 8] = np.asarray(ffn_norm[l], f32).reshape(8, 128).T
        par[l, :, PB_LNG:PB_LNG + 512] = np.asarray(sgu_ln_g[l], f32)[None, :]
        par[l, :, PB_LNB:PB_LNB + 512] = np.asarray(sgu_ln_b[l], f32)[None, :]
        par[l, :, PB_SGUB:PB_SGUB + 512] = np.asarray(sgu_b[l], f32).reshape(1, 512)
        par[l, :, PB_DNN:PB_DNN + 128] = np.asarray(dn_norm[l], f32)[None, :]
        par[l, :, PB_SINK:PB_SINK + 8] = np.asarray(attn_sinks[l], f32)[None, :]
        par[l, :, PB_ALOG:PB_ALOG + 4] = np.asarray(dn_a_log[l], f32)[None, :]
        par[l, :, PB_DTB:PB_DTB + 4] = np.asarray(dn_dt_bias[l], f32)[None, :]
        cw = np.asarray(dn_conv_w[l], f32)
        par[l, :, PB_CONV:PB_CONV + 48] = cw.T.reshape(12, 128, 4).transpose(1, 0, 2).reshape(128, 48)
    fin = np.ascontiguousarray(np.broadcast_to(np.asarray(final_norm, f32)[None, :], (128, D)))
    cst = np.zeros((128, NCONST), f32)
    p = np.arange(128)[:, None]
    f = np.arange(128)[None, :]
    cst[:, CB_ID:CB_ID + 128] = (p == f)
    cst[:, CB_TRIL:CB_TRIL + 128] = (f <= p)
    cst[:, CB_TRIU:CB_TRIU + 128] = (f >= p)
    cst[:, CB_STRL:CB_STRL + 128] = (f < p)
    cst[:, CB_STRU:CB_STRU + 128] = (f > p)
    cst[:, CB_ONES:CB_ONES + 128] = 1.0
    inv_freq = (500000.0 ** (-np.arange(0, 16, 2, dtype=np.float32) / np.float32(16))).astype(f32)
    cst[:, CB_FREQ:CB_FREQ + 8] = inv_freq[None, :]
    shared = dict(par=par, fin=fin, cst=cst, w_in=np.asarray(w_in, f32), sgu_w=np.asarray(sgu_w, f32),
                  w_branch=np.asarray(w_branch, f32), w_out=np.asarray(w_out, f32),
                  w_gate_up=np.asarray(w_gate_up, f32), w_down=np.asarray(w_down, f32))
    per_b = []
    for b in range(B):
        pos = np.asarray(positions[b], np.int32).reshape(NT // 128, 128).T
        d = dict(shared)
        d["x"] = np.ascontiguousarray(np.asarray(x[b], f32))
        d["pos"] = np.ascontiguousarray(pos)
        per_b.append(d)
    return per_b


_CACHE = {}


def kernel(**inputs):
    per_b = host_inputs(**inputs)
    B = len(per_b)
    NT = per_b[0]["x"].shape[0]
    if NT not in _CACHE:
        _CACHE[NT] = build(NT)[0]
    nc = _CACHE[NT]
    n = 8
    in_maps = [per_b[c % B] for c in range(n)]
    res = run_bass_kernel_spmd(nc, in_maps, core_ids=list(range(n)))
    out = np.stack([np.asarray(res.results[b]["y"], np.float32) for b in range(B)], axis=0)
    return out
```
